# Optimizing a Trainium2 kernel written in Bass

```python
import jax, jax.numpy as jnp
from jax import lax
import numpy as np

D_MODEL = 1024
BATCH = 1
SEQ = 16384
DEPTH = 1

RET_HEADS = 4
RET_DK = 128
RET_DV = 256
RET_CHUNK = 128
RET_QK_W = RET_HEADS * RET_DK
RET_V_W = RET_HEADS * RET_DV

ATT_Q_HEADS = 16
ATT_KV_HEADS = 4
ATT_HEAD_DIM = 64
ATT_Q_W = ATT_Q_HEADS * ATT_HEAD_DIM
ATT_KV_W = ATT_KV_HEADS * ATT_HEAD_DIM
WINDOW = 128
ATT_BLOCK = 128

SPLIT_SIZES = (RET_QK_W, RET_QK_W, RET_V_W, RET_V_W,
               ATT_Q_W, ATT_KV_W, ATT_KV_W, ATT_Q_W,
               D_MODEL, D_MODEL)
D_IN = RET_QK_W * 2 + RET_V_W * 2 + ATT_Q_W * 2 + ATT_KV_W * 2 + D_MODEL * 2

RMS_EPS = 1e-6
GN_EPS = 1e-6

kernel_name = "hybrid_retention_window_gqa_gated"


def _rmsnorm(x, g):
    xf = x.astype(jnp.float32)
    y = xf * lax.rsqrt(jnp.mean(xf * xf, axis=-1, keepdims=True) + RMS_EPS)
    return (y * g.astype(jnp.float32)).astype(x.dtype)


def _group_norm(y):
    mu = jnp.mean(y, axis=-1, keepdims=True)
    var = jnp.mean(jnp.square(y - mu), axis=-1, keepdims=True)
    return (y - mu) * lax.rsqrt(var + GN_EPS)


def _split_columns(proj):
    parts = []
    start = 0
    for size in SPLIT_SIZES:
        parts.append(proj[..., start:start + size])
        start += size
    return parts


def _retention_dir(q, k, v, log_gamma, include_diag):
    B, S, H, dk = q.shape
    dv = v.shape[-1]
    C = RET_CHUNK
    n = S // C
    f32 = jnp.float32
    qc = q.astype(f32).reshape(B, n, C, H, dk)
    kc = k.astype(f32).reshape(B, n, C, H, dk)
    vc = v.astype(f32).reshape(B, n, C, H, dv)
    lg = log_gamma.astype(f32)
    pos = jnp.arange(C, dtype=f32)
    diff = pos[:, None] - pos[None, :]
    mask = (diff >= 0) if include_diag else (diff > 0)
    decay_intra = jnp.where(mask[None], jnp.exp(lg[:, None, None] * jnp.maximum(diff, 0.0)[None]), 0.0)
    scores = jnp.einsum('bnihd,bnjhd->bnhij', qc, kc) * decay_intra[None, None]
    intra = jnp.einsum('bnhij,bnjhe->bnihe', scores, vc)
    k_dec = jnp.exp(lg[None, :] * (C - 1 - pos)[:, None])
    kv = jnp.einsum('bnjhd,jh,bnjhe->nbhde', kc, k_dec, vc)
    chunk_decay = jnp.exp(lg * C)[None, :, None, None]

    def step(state, kv_c):
        return state * chunk_decay + kv_c, state

    _, prev = lax.scan(step, jnp.zeros((B, H, dk, dv), f32), kv)
    q_dec = jnp.exp(lg[None, :] * (pos + 1.0)[:, None])
    cross = jnp.einsum('bnihd,ih,nbhde->bnihe', qc, q_dec, prev)
    return (intra + cross).reshape(B, S, H, dv)


def _bidir_retention(q, k, v, log_decay):
    fwd = _retention_dir(q, k, v, log_decay[0], True)
    bwd = _retention_dir(jnp.flip(q, 1), jnp.flip(k, 1), jnp.flip(v, 1), log_decay[1], False)
    return fwd + jnp.flip(bwd, 1)


def _window_attention(q, k, v, sink):
    B, S, Hq, d = q.shape
    Hkv = k.shape[2]
    G = Hq // Hkv
    L = ATT_BLOCK
    n = S // L
    f32 = jnp.float32
    slopes = jnp.exp2(-8.0 * (jnp.arange(Hq, dtype=f32) + 1.0) / Hq).reshape(Hkv, G)
    sink_b = sink.astype(f32).reshape(Hkv, G)[None, :, :, None, None]
    kp = jnp.pad(k, ((0, 0), (L, L), (0, 0), (0, 0)))
    vp = jnp.pad(v, ((0, 0), (L, L), (0, 0), (0, 0)))
    qpos = jnp.arange(L)
    kofs = jnp.arange(3 * L) - L
    rel = jnp.abs(kofs[None, :] - qpos[:, None])
    alibi = -slopes[:, :, None, None] * rel.astype(f32)[None, None]
    scale = d ** -0.5

    def block(i):
        qi = lax.dynamic_slice_in_dim(q, i * L, L, axis=1).reshape(B, L, Hkv, G, d)
        ki = lax.dynamic_slice_in_dim(kp, i * L, 3 * L, axis=1)
        vi = lax.dynamic_slice_in_dim(vp, i * L, 3 * L, axis=1)
        s = jnp.einsum('bqhgd,bkhd->bhgqk', qi, ki).astype(f32) * scale + alibi[None]
        abs_k = i * L + kofs
        valid = (rel <= WINDOW) & ((abs_k >= 0) & (abs_k < S))[None, :]
        s = jnp.where(valid[None, None, None], s, -jnp.inf)
        m = jnp.maximum(jnp.max(s, axis=-1, keepdims=True), sink_b)
        p = jnp.exp(s - m)
        denom = jnp.sum(p, axis=-1, keepdims=True) + jnp.exp(sink_b - m)
        o = jnp.einsum('bhgqk,bkhd->bqhgd', (p / denom).astype(v.dtype), vi)
        return o.reshape(B, L, Hq * d)

    out = lax.map(block, jnp.arange(n))
    return jnp.transpose(out, (1, 0, 2, 3)).reshape(B, S, Hq * d)


def _layer(x, norm_g, w_in, ret_log_decay, q_norm_g, k_norm_g, attn_sink, w_ret_o, w_attn_o, w_out):
    B, S, _ = x.shape
    h = _rmsnorm(x, norm_g)
    proj = jnp.einsum('bsd,de->bse', h, w_in)
    rq, rk, rv, rg, aq, ak, av, ag, mr, ma = _split_columns(proj)
    rq = rq.reshape(B, S, RET_HEADS, RET_DK)
    rk = rk.reshape(B, S, RET_HEADS, RET_DK) * (RET_DK ** -0.5)
    rv = rv.reshape(B, S, RET_HEADS, RET_DV)
    ret = _group_norm(_bidir_retention(rq, rk, rv, ret_log_decay))
    ret = ret.reshape(B, S, RET_V_W).astype(x.dtype)
    y_r = jnp.einsum('bse,ed->bsd', jax.nn.silu(rg) * ret, w_ret_o)
    aq = _rmsnorm(aq.reshape(B, S, ATT_Q_HEADS, ATT_HEAD_DIM), q_norm_g)
    ak = _rmsnorm(ak.reshape(B, S, ATT_KV_HEADS, ATT_HEAD_DIM), k_norm_g)
    av = av.reshape(B, S, ATT_KV_HEADS, ATT_HEAD_DIM)
    att = _window_attention(aq, ak, av, attn_sink)
    y_a = jnp.einsum('bse,ed->bsd', jax.nn.silu(ag) * att, w_attn_o)
    merged = jax.nn.sigmoid(mr) * y_r + jax.nn.sigmoid(ma) * y_a
    return x + jnp.einsum('bsd,de->bse', merged, w_out)


def setup_inputs(seed: int = 0) -> dict:
    key = jax.random.key(seed)
    ks = jax.random.split(key, 11)
    f32 = jnp.float32
    x = jax.random.normal(ks[0], (BATCH, SEQ, D_MODEL), f32)
    norm_g = 1.0 + 0.02 * jax.random.normal(ks[1], (DEPTH, D_MODEL), f32)
    w_in = jax.random.normal(ks[2], (DEPTH, D_MODEL, D_IN), f32) * D_MODEL ** -0.5
    base = jnp.log(1.0 - jnp.exp2(-5.0 - jnp.arange(RET_HEADS, dtype=f32)))
    ret_log_decay = base[None, None, :] * jnp.exp(0.1 * jax.random.normal(ks[3], (DEPTH, 2, RET_HEADS), f32))
    q_norm_g = 1.0 + 0.02 * jax.random.normal(ks[4], (DEPTH, ATT_HEAD_DIM), f32)
    k_norm_g = 1.0 + 0.02 * jax.random.normal(ks[5], (DEPTH, ATT_HEAD_DIM), f32)
    attn_sink = 0.5 * jax.random.normal(ks[6], (DEPTH, ATT_Q_HEADS), f32)
    w_ret_o = jax.random.normal(ks[7], (DEPTH, RET_V_W, D_MODEL), f32) * RET_V_W ** -0.5
    w_attn_o = jax.random.normal(ks[8], (DEPTH, ATT_Q_W, D_MODEL), f32) * ATT_Q_W ** -0.5
    w_out = jax.random.normal(ks[9], (DEPTH, D_MODEL, D_MODEL), f32) * D_MODEL ** -0.5
    return {"x": x, "norm_g": norm_g, "w_in": w_in, "ret_log_decay": ret_log_decay,
            "q_norm_g": q_norm_g, "k_norm_g": k_norm_g, "attn_sink": attn_sink,
            "w_ret_o": w_ret_o, "w_attn_o": w_attn_o, "w_out": w_out}


def reference(x, norm_g, w_in, ret_log_decay, q_norm_g, k_norm_g, attn_sink, w_ret_o, w_attn_o, w_out):
    for l in range(DEPTH):
        x = _layer(x, norm_g[l], w_in[l], ret_log_decay[l], q_norm_g[l], k_norm_g[l],
                   attn_sink[l], w_ret_o[l], w_attn_o[l], w_out[l])
    return x
```

```python
import os
from contextlib import ExitStack
import numpy as np
import concourse.bass as bass
import concourse.mybir as mybir
from concourse.bass_utils import run_bass_kernel_spmd

F32 = mybir.dt.float32
BF16 = mybir.dt.bfloat16
U8 = mybir.dt.uint8
AF = mybir.ActivationFunctionType
ALU = mybir.AluOpType
AX = mybir.AxisListType

NCORES = 8
SEQ = 16384
DM = 1024
TOK = SEQ // NCORES
NT = SEQ // 128
NOWN = TOK // 128
D_IN = 7680
C_RQ, C_RK, C_RV, C_RG, C_AQ, C_AK, C_AV, C_AG, C_MR, C_MA = 0, 512, 1024, 2048, 3072, 4096, 4352, 4608, 5632, 6656
RMS_EPS = 1e-6
GN_EPS = 1e-6

ENG_NAMES = ("pe", "act", "dve", "pool", "sp")
SAME_ENGINE_KINDS = ("raw", "waw", "war")


class Op:
    __slots__ = ("eng", "fn", "deps", "inc_val", "need_inc", "is_dma", "dma_sem", "dma_val", "waits", "clock")

    def __init__(self, eng, fn, is_dma=False):
        self.eng = eng
        self.fn = fn
        self.deps = []
        self.inc_val = None
        self.need_inc = False
        self.is_dma = is_dma
        self.dma_sem = None
        self.dma_val = None


class Sched:
    def __init__(self, n_dma_sems=8):
        self.ops = {e: [] for e in ENG_NAMES}
        self.last_writer = {}
        self.readers = {}
        self.n_dma_sems = n_dma_sems
        self.dma_rr = {e: 0 for e in ENG_NAMES}
        self.dma_cnt = {}
        self.dma_last = {}
        self.bank_last = {}
        self.order = []

    def _track(self, op, reads, writes, banks=()):
        for b in banks:
            p = self.bank_last.get(b)
            if p is not None and p.eng != op.eng:
                op.deps.append((p, "bank"))
            self.bank_last[b] = op
        for k in reads:
            w = self.last_writer.get(k)
            if w is not None:
                op.deps.append((w, "raw"))
        for k in writes:
            w = self.last_writer.get(k)
            if w is not None:
                op.deps.append((w, "waw"))
            for r in self.readers.get(k, ()):
                if r is not op:
                    op.deps.append((r, "war"))
        for k in reads:
            self.readers.setdefault(k, []).append(op)
        for k in writes:
            self.last_writer[k] = op
            self.readers[k] = []

    def op(self, eng, fn, reads=(), writes=(), banks=()):
        o = Op(eng, fn)
        self._track(o, reads, writes, banks)
        self.ops[eng].append(o)
        self.order.append(o)
        return o

    def dma(self, eng, fn, reads=(), writes=()):
        o = Op(eng, fn, is_dma=True)
        self._track(o, reads, writes)
        slot = (eng, self.dma_rr[eng] % self.n_dma_sems)
        self.dma_rr[eng] += 1
        prev = self.dma_last.get(slot)
        if prev is not None:
            o.deps.append((prev, "raw"))
        self.dma_cnt[slot] = self.dma_cnt.get(slot, 0) + 1
        o.dma_sem = slot
        o.dma_val = 16 * self.dma_cnt[slot]
        self.dma_last[slot] = o
        self.ops[eng].append(o)
        self.order.append(o)
        return o

    def barrier(self):
        lasts = []
        for e in ENG_NAMES:
            for o in reversed(self.ops[e]):
                if not o.is_dma:
                    lasts.append(o)
                    break
        lasts += list(self.dma_last.values())
        for e in ENG_NAMES:
            o = Op(e, lambda h: h.nop())
            for p in lasts:
                o.deps.append((p, "raw"))
            self.ops[e].append(o)
            self.order.append(o)
        self.last_writer = {}
        self.readers = {}
        self.bank_last = {}

    def finalize(self):
        for e in ENG_NAMES:
            for o in self.ops[e]:
                for (p, kind) in o.deps:
                    if p.is_dma:
                        continue
                    if p.eng == o.eng and (p.eng == "pe" or kind not in SAME_ENGINE_KINDS):
                        continue
                    p.need_inc = True
        for e in ENG_NAMES:
            c = 0
            for o in self.ops[e]:
                if o.is_dma:
                    continue
                if o.need_inc:
                    c += 1
                    o.inc_val = c

        known = {e: {} for e in ENG_NAMES}
        for o in self.order:
            kn = known[o.eng]
            need = {}
            for (p, kind) in o.deps:
                if p.is_dma:
                    k, v = p.dma_sem, p.dma_val
                else:
                    if p.eng == o.eng and (p.eng == "pe" or kind not in SAME_ENGINE_KINDS):
                        continue
                    k, v = ("eng", p.eng), p.inc_val
                if v > need.get(k, (0, None))[0]:
                    need[k] = (v, p)
            waits = []
            for k, (v, p) in need.items():
                if kn.get(k, 0) >= v:
                    continue
                waits.append((k, v))
                for kk, vv in p.clock.items():
                    if vv > kn.get(kk, 0):
                        kn[kk] = vv
            o.waits = waits
            clk = dict(kn)
            if o.is_dma:
                clk[o.dma_sem] = max(clk.get(o.dma_sem, 0), o.dma_val)
            elif o.need_inc:
                clk[("eng", o.eng)] = max(clk.get(("eng", o.eng), 0), o.inc_val)
            o.clock = clk

    def sem_keys(self):
        return [("eng", e) for e in ENG_NAMES] + sorted(self.dma_cnt.keys())

    def emit_engine(self, e, handle, semmap):
        for o in self.ops[e]:
            for (k, v) in o.waits:
                handle.wait_ge(semmap[k], v)
            ins = o.fn(handle)
            if o.is_dma:
                ins.then_inc(semmap[o.dma_sem], 16)
            elif o.need_inc:
                ins.then_inc(semmap[("eng", e)], 1)


def _const_tables():
    p = np.arange(128, dtype=np.float64)[:, None]
    i = np.arange(128, dtype=np.float64)[None, :]
    t = {}
    t["posf"] = SEQ - 1 - (128 * i + p)
    t["posb"] = (128 * i + p) - TOK
    t["ip1"] = np.broadcast_to(i + 1, (128, 128))
    t["cmi"] = np.broadcast_to(128 - i, (128, 128))
    t["dpos"] = np.maximum(i - p, 0)
    t["dneg"] = np.maximum(p - i, 0)
    t["mge"] = (i >= p).astype(np.float64)
    t["mlt"] = (p > i).astype(np.float64)
    t["ar0"] = i + 128 - p
    t["ar1"] = np.abs(p - i)
    t["ar2"] = p + 128 - i
    t["am0"] = (p >= i).astype(np.float64)
    t["am1"] = np.ones((128, 128))
    t["am2"] = (p <= i).astype(np.float64)
    t["ident"] = np.eye(128)
    t["oblk"] = ((p // 64) == (i // 64)).astype(np.float64)
    t["ones"] = np.ones((128, 128))
    names = list(t.keys())
    arr = np.concatenate([np.asarray(t[n], dtype=np.float32) for n in names], axis=1)
    vec = np.zeros((128, 8), np.float32)
    vec[:, 0] = 127 - np.arange(128)
    vec[:, 1] = np.arange(128)
    vec[:, 2] = -0.5
    return names, np.ascontiguousarray(arr), vec


CONST_NAMES, CONST_ARR, CONST_VEC = _const_tables()
NCONST = CONST_ARR.shape[1]


def _core_masks(c):
    m = np.zeros((128, 128 + 128 + 8 + 8 + 2), np.float32)
    for tile in range(NOWN, NT):
        g = tile // NOWN
        after = (c + g) < NCORES
        m[:, tile] = 0.0 if after else 1.0
        m[:, 128 + tile] = 1.0 if after else 0.0
    for g in range(1, 8):
        after = (c + g) < NCORES
        m[:, 256 + g] = 0.0 if after else 1.0
        m[:, 264 + g] = 1.0 if after else 0.0
    m[:, 272] = 1.0 if c > 0 else 0.0
    m[:, 273] = 1.0 if c < NCORES - 1 else 0.0
    return m


SM_NG, SM_LG, SM_SINK, SM_GQ, SM_GK = 0, 1024, 1032, 1048, 1049
NSMALL = 1050


def build(stage_limit=99, debug=False):
    nc = bass.Bass("TRN2", target_bir_lowering=False)
    x_d = nc.dram_tensor("x", [SEQ, DM], F32, kind="ExternalInput").ap()
    win_d = nc.dram_tensor("w_in", [DM, D_IN], F32, kind="ExternalInput").ap()
    wro_d = nc.dram_tensor("w_ret_o", [DM, DM], F32, kind="ExternalInput").ap()
    wao_d = nc.dram_tensor("w_attn_o", [DM, DM], F32, kind="ExternalInput").ap()
    wout_d = nc.dram_tensor("w_out", [DM, DM], F32, kind="ExternalInput").ap()
    small_d = nc.dram_tensor("small", [128, NSMALL], F32, kind="ExternalInput").ap()
    const_d = nc.dram_tensor("consts", [128, NCONST], F32, kind="ExternalInput").ap()
    cvec_d = nc.dram_tensor("cvec", [128, 8], F32, kind="ExternalInput").ap()
    cmask_d = nc.dram_tensor("cmask", [128, 274], F32, kind="ExternalInput").ap()
    out_d = nc.dram_tensor("out", [TOK, DM], F32, kind="ExternalOutput").ap()
    dbg_list = []

    win_v = win_d.rearrange("(k p) e -> p k e", p=128)
    wro_v = wro_d.rearrange("(k p) e -> p k e", p=128)
    wao_v = wao_d.rearrange("(k p) e -> p k e", p=128)
    wout_v = wout_d.rearrange("(k p) e -> p k e", p=128)

    es = ExitStack()
    ARENA = 206 * 1024
    arena = es.enter_context(nc.sbuf_tensor("arena", [128, ARENA], U8))
    ps_all = es.enter_context(nc.psum_tensor("ps_all", [128, 4096], F32))
    banks = [ps_all[:, 512 * b:512 * (b + 1)] for b in range(8)]

    def A(off_kb, nbytes, dt, pat=None, **kw):
        off = int(off_kb * 1024)
        v = arena[:, off:off + nbytes].bitcast(dt)
        if pat is not None:
            v = v.rearrange(pat, **kw)
        return v

    S = Sched(n_dma_sems=8)

    hT_own = A(0, 32768, BF16, "p (k t) -> p k t", k=8)
    hT_halo = A(32, 4096, BF16, "p (k t) -> p k t", k=8)
    Sst = A(36, 8192, F32, "p (d h e) -> p d h e", d=2, h=4)
    ident = A(44, 256, BF16)
    ones_bf = A(44.25, 256, BF16)
    oblk = A(44.5, 512, F32)
    lgrep = A(45, 32, F32)
    gC = A(45.03125, 32, F32)
    kdec = A(45.0625, 32, F32)
    cvec = A(45.09375, 32, F32)
    negsink = A(45.125, 64, F32)
    gkq = A(45.1875, 4, F32)
    halov = A(45.25, 8, F32)
    grpm = A(45.3125, 64, F32)
    ones_vb = A(45.5, 256, BF16)
    ones_va = A(45.75, 256, BF16)
    gq_t = A(46, 4, F32)
    gk_t = A(46.0625, 4, F32)
    mergedT = A(48, 32768, BF16, "p (k t) -> p k t", k=8)

    cidx = {n: j for j, n in enumerate(CONST_NAMES)}

    def dbg(name, ap, shape, dt=F32):
        if not debug:
            return
        t = nc.dram_tensor("dbg_" + name, list(shape), dt, kind="ExternalOutput").ap()
        dbg_list.append(("dbg_" + name, ap, t))

    k_tok = A(80, 16384, BF16, "p (n c) -> p n c", n=NOWN)
    v_tok = A(96, 32768, BF16, "p (n c) -> p n c", n=NOWN)
    w_kv = A(128, 24576, BF16, "p (k c) -> p k c", k=8)
    xbuf = [A(160 + 4 * j, 4096, F32) for j in range(3)]
    sqs = A(172, 2048, BF16)
    NHN = 5
    hn = [A(174 + 2 * j, 2048, BF16) for j in range(NHN)]
    hT_r = [A(184 + 2 * j, 2048, BF16, "p (k t) -> p k t", k=8) for j in range(2)]
    k_sb = [A(188 + 0.5 * j, 512, BF16) for j in range(2)]
    kw = [A(189 + 0.5 * j, 512, BF16) for j in range(2)]
    v_dec = [A(190 + j, 1024, BF16) for j in range(2)]
    W_all = A(192, 2048, F32, "p (t h) -> p t h", h=4)
    g_tile = A(194, 4096, F32)
    cmask = A(158.5, 274 * 4, F32)
    ss = A(198, 16, F32)
    ms = A(198.0625, 16, F32)
    rstd = A(198.125, 16, F32)
    Jb = A(156.5, 2048, BF16)
    JT = A(152, 2048, BF16, "p (k t) -> p k t", k=8)
    small_sb = A(152, NSMALL * 4, F32)
    E1 = A(203, 2048, F32, "p (t h) -> p t h", h=4)
    ctab = A(199, 8 * 512, F32, "p (n c) -> p n c", n=8)
    E2 = A(156.5, 2048, F32, "p (t h) -> p t h", h=4)

    S.dma("sp", lambda e: e.dma_start(out=small_sb, in_=small_d), writes=["small"])
    S.dma("sp", lambda e: e.dma_start(out=ctab, in_=const_d[:, 0:1024].rearrange("p (n c) -> p n c", n=8)),
          writes=["ctab"])
    S.dma("sp", lambda e: e.dma_start(out=cmask, in_=cmask_d), writes=["cmask"])
    S.dma("sp", lambda e: e.dma_start(out=cvec, in_=cvec_d), writes=["cvec"])
    S.dma("sp", lambda e: e.dma_start(
        out=oblk, in_=const_d[:, cidx["oblk"] * 128:(cidx["oblk"] + 1) * 128]), writes=["oblk"])
    S.dma("pool", lambda e: e.dma_start(
        out=ident, in_=const_d[:, cidx["ident"] * 128:(cidx["ident"] + 1) * 128]), writes=["ident"])
    S.dma("pool", lambda e: e.dma_start(
        out=ones_bf, in_=const_d[:, cidx["ones"] * 128:(cidx["ones"] + 1) * 128]), writes=["ones_bf"])
    S.dma("pool", lambda e: e.dma_start(out=w_kv[:, :, 0:512], in_=win_v[:, :, C_RK:C_RK + 512]),
          writes=[("w_kv", 0)])
    S.dma("pool", lambda e: e.dma_start(out=w_kv[:, :, 512:1536], in_=win_v[:, :, C_RV:C_RV + 1024]),
          writes=[("w_kv", 1)])

    S.op("dve", lambda e: e.tensor_copy(out=g_tile, in_=small_sb[:, SM_NG:SM_NG + 1024]),
         reads=["small"], writes=["g_tile"])
    S.op("dve", lambda e: e.tensor_copy(out=lgrep, in_=small_sb[:, SM_LG:SM_LG + 8]),
         reads=["small"], writes=["lgrep"])
    S.op("dve", lambda e: e.tensor_scalar(out=negsink, in0=small_sb[:, SM_SINK:SM_SINK + 16], scalar1=-1.0,
                                          scalar2=None, op0=ALU.mult), reads=["small"], writes=["negsink"])
    S.op("dve", lambda e: e.scalar_tensor_tensor(out=gkq, in0=small_sb[:, SM_GQ:SM_GQ + 1], scalar=0.125,
                                                 in1=small_sb[:, SM_GK:SM_GK + 1], op0=ALU.mult, op1=ALU.mult),
         reads=["small"], writes=["gkq"])
    S.op("dve", lambda e: e.tensor_copy(out=halov, in_=cmask[:, 272:274]), reads=["cmask"], writes=["halov"])
    S.op("dve", lambda e: e.tensor_copy(out=grpm, in_=cmask[:, 256:272]), reads=["cmask"], writes=["grpm"])
    S.op("dve", lambda e: e.tensor_scalar(out=ones_vb, in0=ones_bf, scalar1=halov[:, 0:1], scalar2=None,
                                          op0=ALU.mult), reads=["ones_bf", "halov"], writes=["ones_vb"])
    S.op("dve", lambda e: e.tensor_scalar(out=ones_va, in0=ones_bf, scalar1=halov[:, 1:2], scalar2=None,
                                          op0=ALU.mult), reads=["ones_bf", "halov"], writes=["ones_va"])
    S.op("dve", lambda e: e.memset(Sst, 0.0), writes=["Sst"])
    S.op("act", lambda e: e.activation(out=gC, in_=lgrep, func=AF.Exp, scale=128.0),
         reads=["lgrep"], writes=["gC"])
    S.op("act", lambda e: e.activation(out=kdec[:, 0:4], in_=lgrep[:, 0:4], func=AF.Exp, scale=cvec[:, 0:1]),
         reads=["lgrep", "cvec"], writes=["kdec"])
    S.op("act", lambda e: e.activation(out=kdec[:, 4:8], in_=lgrep[:, 4:8], func=AF.Exp, scale=cvec[:, 1:2]),
         reads=["lgrep", "cvec"], writes=["kdec"])
    for h in range(4):
        S.op("act", lambda e, h=h: e.activation(out=E1[:, NOWN:NT, h], in_=ctab[:, 0, NOWN:NT], func=AF.Exp,
                                                scale=lgrep[:, h:h + 1]),
             reads=["ctab", "lgrep"], writes=[("E1", h)])
        S.op("act", lambda e, h=h: e.activation(out=E2[:, NOWN:NT, h], in_=ctab[:, 1, NOWN:NT], func=AF.Exp,
                                                scale=lgrep[:, 4 + h:5 + h]),
             reads=["ctab", "lgrep"], writes=[("E2", h)])
    rk = [("E1", h) for h in range(4)] + [("E2", h) for h in range(4)]
    S.op("dve", lambda e: e.tensor_tensor(
        out=E1[:, NOWN:NT, :], in0=E1[:, NOWN:NT, :],
        in1=cmask[:, NOWN:NT].unsqueeze(2).to_broadcast([128, NT - NOWN, 4]), op=ALU.mult),
        reads=rk + ["cmask"], writes=["E1m"])
    S.op("dve", lambda e: e.tensor_tensor(
        out=E2[:, NOWN:NT, :], in0=E2[:, NOWN:NT, :],
        in1=cmask[:, 128 + NOWN:128 + NT].unsqueeze(2).to_broadcast([128, NT - NOWN, 4]), op=ALU.mult),
        reads=rk + ["cmask"], writes=["E2m"])
    S.op("dve", lambda e: e.tensor_tensor(out=W_all[:, NOWN:NT, :], in0=E1[:, NOWN:NT, :],
                                          in1=E2[:, NOWN:NT, :], op=ALU.add),
         reads=["E1m", "E2m"], writes=["W_all"])

    S.op("dve", lambda e: e.tensor_scalar(out=W_all[:, NOWN:NT, 0:2], in0=W_all[:, NOWN:NT, 0:2],
                                          scalar1=float(128 ** -0.5), scalar2=None, op0=ALU.mult),
         reads=["W_all"], writes=["W_all"])
    PS_TP, PS_K, PS_V0, PS_V1 = 0, 1, 2, 3
    PS_T = 3
    PS_J = [[4, 5], [6, 7]]
    tp_bf = banks[PS_TP][:].bitcast(BF16).rearrange("p (k t) -> p k t", k=8)
    order = list(range(NT))
    if stage_limit == 0:
        order = order[:int(os.environ.get("K_NT0", "20"))]
    grp_cnt = {g: 0 for g in range(1, 8)}
    grp_tot = {g: sum(1 for t in order if t >= NOWN and t // NOWN == g) for g in range(1, 8)}
    run_first = [True]

    def front(s):
        i = order[s]
        xb = xbuf[s % 3]
        S.dma("sp", lambda e: e.dma_start(out=xb, in_=x_d[128 * i:128 * (i + 1), :]), writes=[("x", s % 3)])
        S.op("act", lambda e: e.activation(out=sqs, in_=xb, func=AF.Square, accum_out=ss[:, s % 4:s % 4 + 1]),
             reads=[("x", s % 3)], writes=["sqs", ("ss", s % 4)])
        S.op("dve", lambda e: e.tensor_scalar(out=ms[:, s % 4:s % 4 + 1], in0=ss[:, s % 4:s % 4 + 1],
                                              scalar1=1.0 / DM, scalar2=RMS_EPS, op0=ALU.mult, op1=ALU.add),
             reads=[("ss", s % 4)], writes=[("ms", s % 4)])
        S.op("pool", lambda e: e.tensor_tensor(out=rstd[:, s % 4:s % 4 + 1], in0=ms[:, s % 4:s % 4 + 1],
                                               in1=cvec[:, 2:3], op=ALU.pow),
             reads=[("ms", s % 4), "cvec"], writes=[("rstd", s % 4)])
        S.op("dve", lambda e: e.scalar_tensor_tensor(out=hn[s % NHN], in0=xb, scalar=rstd[:, s % 4:s % 4 + 1],
                                                     in1=g_tile, op0=ALU.mult, op1=ALU.mult),
             reads=[("x", s % 3), ("rstd", s % 4), "g_tile"], writes=[("hn", s % NHN)])

    def hT_dst(s):
        i = order[s]
        if i < NOWN:
            return hT_own[:, :, 128 * i:128 * (i + 1)], [("hT_own", i)]
        return hT_r[s % 2], [("hT_r", s % 2)]

    def mid(s):
        i = order[s]
        for k in range(8):
            S.op("pe", lambda e, k=k: e.transpose(out=tp_bf[:, k, :], in_=hn[s % NHN][:, 128 * k:128 * (k + 1)],
                                                  identity=ident),
                 reads=[("hn", s % NHN), "ident"], writes=[("ps", PS_TP)])
        dst, keys = hT_dst(s)
        S.op("act", lambda e: e.activation(out=dst, in_=tp_bf, func=AF.Copy), reads=[("ps", PS_TP)], writes=keys)
        if i == NOWN:
            S.op("pool", lambda e: e.tensor_copy(out=hT_halo[:, :, 128:256], in_=dst), reads=keys,
                 writes=[("hT_halo", 1)])
        if i == NT - 1:
            S.op("pool", lambda e: e.tensor_copy(out=hT_halo[:, :, 0:128], in_=dst), reads=keys,
                 writes=[("hT_halo", 0)])

    def proj(s):
        i = order[s]
        src, keys = hT_dst(s)
        scl0 = float(128 ** -0.5)
        if i < NOWN:
            plan = ((PS_K, 0), (PS_V0, 512), (PS_V1, 1024))
        else:
            plan = ((PS_K, 0), (PS_V0, 1024))
        for (bank, c0) in plan:
            for k in range(8):
                S.op("pe", lambda e, k=k, bank=bank, c0=c0: e.matmul(
                    banks[bank][:], lhsT=src[:, k, :], rhs=w_kv[:, k, c0:c0 + 512], start=(k == 0), stop=(k == 7)),
                    reads=keys + [("w_kv", 0 if c0 == 0 else 1)], writes=[("ps", bank)])
        if i < NOWN:
            S.op("act", lambda e: e.activation(out=k_tok[:, i, :], in_=banks[PS_K][:], func=AF.Copy, scale=scl0),
                 reads=[("ps", PS_K)], writes=[("k_tok", i)])
            S.op("act", lambda e: e.activation(out=v_tok[:, i, 0:512], in_=banks[PS_V0][:], func=AF.Copy),
                 reads=[("ps", PS_V0)], writes=[("v_tok", i, 0)])
            S.op("dve", lambda e: e.tensor_copy(out=v_tok[:, i, 512:1024], in_=banks[PS_V1][:]),
                 reads=[("ps", PS_V1)], writes=[("v_tok", i, 1)])
        else:
            S.op("act", lambda e: e.activation(out=k_sb[s % 2], in_=banks[PS_K][:, 256:512], func=AF.Copy, scale=scl0),
                 reads=[("ps", PS_K)], writes=[("k_sb", s % 2)])
            for h in range(2):
                S.op("act", lambda e, h=h: e.activation(
                    out=kw[s % 2][:, 128 * h:128 * (h + 1)], in_=banks[PS_K][:, 128 * h:128 * (h + 1)], func=AF.Copy,
                    scale=W_all[:, i, h:h + 1]), reads=[("ps", PS_K), "W_all"], writes=[("kw", s % 2)])
            S.op("dve", lambda e: e.tensor_tensor(
                out=v_dec[s % 2].rearrange("p (h e) -> p h e", h=2),
                in0=banks[PS_V0][:].rearrange("p (h e) -> p h e", h=2),
                in1=W_all[:, i, 2:4].unsqueeze(2).to_broadcast([128, 2, 256]), op=ALU.mult),
                reads=[("ps", PS_V0), "W_all"], writes=[("v_dec", s % 2)])

    NS = len(order)
    run_end = {}
    for s in range(NS):
        i = order[s]
        if i >= NOWN:
            g = i // NOWN
            nxt = order[s + 1] if s + 1 < NS else None
            run_end[s] = (nxt is None) or (nxt < NOWN) or (nxt // NOWN != g)

    def kvacc2(s):
        i = order[s]
        if i < NOWN:
            return
        g = i // NOWN
        st = run_first[0]
        run_first[0] = False
        for h in range(2):
            S.op("pe", lambda e, h=h, st=st: e.matmul(
                banks[PS_T][:, 256 * h:256 * (h + 1)], lhsT=k_sb[s % 2][:, 128 * h:128 * (h + 1)],
                rhs=v_dec[s % 2][:, 256 * h:256 * (h + 1)], start=(st and h == 0), stop=bool(run_end[s]),
                skip_group_check=True),
                reads=[("k_sb", s % 2), ("v_dec", s % 2)], writes=[("ps", PS_T)])
        for h in range(2):
            for half in range(2):
                bank = PS_J[h][half]
                S.op("pe", lambda e, h=h, half=half, bank=bank, st=st: e.matmul(
                    banks[bank][:], lhsT=kw[s % 2][:, 128 * h:128 * (h + 1)],
                    rhs=hn[s % NHN][:, 512 * half:512 * (half + 1)], start=st, stop=bool(run_end[s])),
                    reads=[("kw", s % 2), ("hn", s % NHN)], writes=[("ps", bank)])
        if run_end[s]:
            for d in range(2):
                S.op("dve", lambda e, d=d, g=g: e.scalar_tensor_tensor(
                    out=Sst[:, d, 2:4, :], in0=banks[PS_T][:].rearrange("p (h e) -> p h e", h=2),
                    scalar=grpm[:, 8 * d + g:8 * d + g + 1], in1=Sst[:, d, 2:4, :],
                    op0=ALU.mult, op1=ALU.add),
                    reads=[("ps", PS_T), "grpm", "Sst"], writes=["Sst"])
            for h in range(2):
                for half in range(2):
                    bank = PS_J[h][half]
                    S.op("act", lambda e, half=half, bank=bank: e.activation(
                        out=Jb[:, 512 * half:512 * (half + 1)], in_=banks[bank][:], func=AF.Copy),
                        reads=[("ps", bank), "W_all"], writes=[("Jb", half)])
                for k in range(8):
                    S.op("pe", lambda e, k=k: e.transpose(out=tp_bf[:, k, :], in_=Jb[:, 128 * k:128 * (k + 1)],
                                                          identity=ident),
                         reads=[("Jb", k // 4), "ident"], writes=[("ps", PS_TP)])
                S.op("act", lambda e: e.activation(out=JT, in_=tp_bf, func=AF.Copy),
                     reads=[("ps", PS_TP), "g_tile", "lgrep", "negsink", "gkq"],
                     writes=["JT"])
                for k in range(8):
                    S.op("pe", lambda e, k=k, h=h: e.matmul(
                        banks[PS_V0][:, 0:256], lhsT=JT[:, k, :], rhs=w_kv[:, k, 512 + 256 * h:512 + 256 * (h + 1)],
                        start=(k == 0), stop=(k == 7)), reads=["JT", ("w_kv", 1)], writes=[("ps", PS_V0)])
                for d in range(2):
                    S.op("dve", lambda e, d=d, g=g, h=h: e.scalar_tensor_tensor(
                        out=Sst[:, d, h, :], in0=banks[PS_V0][:, 0:256],
                        scalar=grpm[:, 8 * d + g:8 * d + g + 1], in1=Sst[:, d, h, :],
                        op0=ALU.mult, op1=ALU.add),
                        reads=[("ps", PS_V0), "grpm", "Sst"], writes=["Sst"])
            run_first[0] = True

    for s in range(NS + 3):
        if s < NS:
            front(s)
        if 1 <= s and s - 1 < NS:
            mid(s - 1)
        if 2 <= s and s - 2 < NS:
            proj(s - 2)
        if 3 <= s and s - 3 < NS:
            kvacc2(s - 3)

    if stage_limit == 0:
        dbg("hT_own", hT_own[:, :, 0:256], [128, 8, 256], BF16)
        dbg("hT_halo", hT_halo, [128, 8, 256], BF16)
        dbg("k_tok", k_tok[:, 0:2, :], [128, 2, 512], BF16)
        dbg("v_tok", v_tok[:, 0:2, :], [128, 2, 1024], BF16)
        dbg("Sst", Sst, [128, 2, 4, 256], F32)
        dbg("W_all", W_all, [128, 128, 4], F32)
    outs_written = []

    if stage_limit >= 2:
        S.barrier()
        Pb = A(48, 32768, BF16, "p (n c) -> p n c", n=NOWN)
        retnT = A(128, 32768, BF16, "p (k t) -> p k t", k=8)
        w_qk = A(160, 16384, BF16, "p (k c) -> p k c", k=8)
        qT = A(176, 2048, BF16, "p (h t) -> p h t", h=4)
        qfT = A(178, 2048, BF16, "p (h t) -> p h t", h=4)
        qbT = A(180, 2048, BF16, "p (h t) -> p h t", h=4)
        kT = A(182, 2048, BF16, "p (h t) -> p h t", h=4)
        Dt = A(184, 2048, F32, "p (h t) -> p h t", h=4)
        Rf = A(186, 2048, F32, "p (h t) -> p h t", h=4)
        Rb = A(188, 2048, F32, "p (h t) -> p h t", h=4)
        AT = A(190, 1024, BF16, "p (h t) -> p h t", h=4)
        kfs = A(191, 1024, BF16, "p (h t) -> p h t", h=4)
        kbs = A(192, 1024, BF16, "p (h t) -> p h t", h=4)
        Pf_bf = A(193, 2048, BF16)
        tn = A(195, 2048, BF16)
        st = A(197, 96, F32, "p (h s) -> p h s", h=4)
        mv = A(197.125, 32, F32, "p (h s) -> p h s", h=4)
        rsin = A(197.1875, 16, F32)
        rs = A(197.25, 16, F32)
        nmr = A(197.3125, 16, F32)
        tD1 = A(197.5, 512, F32)
        tD2 = A(198, 512, F32)
        PS_QA, PS_QB, PS_SCTP, PS_KV = 0, 1, 2, 3
        PS_O = [[4, 5], [6, 7]]
        tp2_bf = banks[PS_SCTP][:].bitcast(BF16).rearrange("p (k t) -> p k t", k=8)
        S.dma("pool", lambda e: e.dma_start(out=w_qk, in_=win_v[:, :, 0:1024]), writes=["w_qk"])
        for h in range(4):
            S.op("act", lambda e, h=h: e.activation(out=Rf[:, h, :], in_=ctab[:, 2, :], func=AF.Exp,
                                                    scale=lgrep[:, h:h + 1]), reads=["ctab"], writes=[("Rf", h)])
            S.op("act", lambda e, h=h: e.activation(out=Rb[:, h, :], in_=ctab[:, 3, :], func=AF.Exp,
                                                    scale=lgrep[:, 4 + h:5 + h]), reads=["ctab"], writes=[("Rb", h)])
            S.op("act", lambda e, h=h: e.activation(out=tD1, in_=ctab[:, 4, :], func=AF.Exp,
                                                    scale=lgrep[:, h:h + 1]), reads=["ctab"], writes=["tD1"])
            S.op("act", lambda e, h=h: e.activation(out=tD2, in_=ctab[:, 5, :], func=AF.Exp,
                                                    scale=lgrep[:, 4 + h:5 + h]), reads=["ctab"], writes=["tD2"])
            S.op("dve", lambda e: e.tensor_tensor(out=tD1, in0=tD1, in1=ctab[:, 6, :], op=ALU.mult),
                 reads=["tD1", "ctab"], writes=["tD1"])
            S.op("dve", lambda e: e.tensor_tensor(out=tD2, in0=tD2, in1=ctab[:, 7, :], op=ALU.mult),
                 reads=["tD2", "ctab"], writes=["tD2"])
            S.op("dve", lambda e, h=h: e.tensor_tensor(out=Dt[:, h, :], in0=tD1, in1=tD2, op=ALU.add),
                 reads=["tD1", "tD2"], writes=[("Dt", h)])
        Dkeys = [("Dt", h) for h in range(4)]
        ksb = [kbs, kfs]
        ATb = [AT, kbs]
        Pfb = [Pf_bf, A(199, 2048, BF16)]

        def k_scale(n, d, j):
            S.op("pool", lambda e: e.tensor_tensor(
                out=ksb[j], in0=k_tok[:, n, :].rearrange("p (h t) -> p h t", h=4),
                in1=kdec[:, 4 * d:4 * d + 4].unsqueeze(2).to_broadcast([128, 4, 128]), op=ALU.mult),
                reads=["kdec"], writes=[("ks", j)])

        def kv_mm(n, j, hs, bank_of):
            for h in hs:
                bank = bank_of(h)
                S.op("pe", lambda e, h=h, bank=bank: e.matmul(
                    banks[bank][:, 256 * (h % 2):256 * (h % 2 + 1)], lhsT=ksb[j][:, h, :],
                    rhs=v_tok[:, n, 256 * h:256 * (h + 1)], start=True, stop=True),
                    reads=[("ks", j)], writes=[("pskv", bank, h % 2)], banks=[bank])

        def s_update(d, hs, bank_of):
            for h in hs:
                bank = bank_of(h)
                S.op("dve", lambda e, h=h, bank=bank: e.scalar_tensor_tensor(
                    out=Sst[:, d, h, :], in0=Sst[:, d, h, :], scalar=gC[:, 4 * d + h:4 * d + h + 1],
                    in1=banks[bank][:, 256 * (h % 2):256 * (h % 2 + 1)], op0=ALU.mult, op1=ALU.add),
                    reads=[("pskv", bank, h % 2), ("S", d)], writes=[("S", d)], banks=[bank])

        def pre_mm(n, idx):
            k_scale(n, 1, idx % 2)
            kv_mm(n, idx % 2, range(4), lambda h, idx=idx: PS_O[idx % 2][h // 2])

        pre_mm(NOWN - 1, 0)
        for idx, n in enumerate(range(NOWN - 1, 0, -1)):
            if n - 1 >= 1:
                pre_mm(n - 1, idx + 1)
            S.op("act", lambda e, n=n: e.activation(out=Pb[:, n, :], in_=Sst[:, 1].rearrange("p h e -> p (h e)"),
                                                    func=AF.Copy), reads=[("S", 1)], writes=[("Pb", n)])
            s_update(1, range(4), lambda h, idx=idx: PS_O[idx % 2][h // 2])
        S.op("act", lambda e: e.activation(out=Pb[:, 0, :], in_=Sst[:, 1].rearrange("p h e -> p (h e)"),
                                           func=AF.Copy), reads=[("S", 1)], writes=[("Pb", 0)])

        scl = float(128 ** -0.5)
        PS_QQ = [0, 1]
        PS_SC = PS_TP = 2
        tpz_bf = banks[PS_TP][:].bitcast(BF16).rearrange("p (k t) -> p k t", k=8)

        def proj_block(b):
            tsl = slice(256 * b, 256 * (b + 1))
            for h in range(4):
                qb_ = PS_QQ[h % 2]
                for (c0, off) in ((C_RQ + 128 * h, 0), (C_RK + 128 * h, 256)):
                    for k in range(8):
                        S.op("pe", lambda e, k=k, c0=c0, off=off, qb_=qb_: e.matmul(
                            banks[qb_][:, off:off + 256], lhsT=w_qk[:, k, c0:c0 + 128], rhs=hT_own[:, k, tsl],
                            start=(k == 0), stop=(k == 7)), reads=["w_qk"], writes=[("psqk", h % 2)], banks=[qb_])
                S.op("act", lambda e, h=h, qb_=qb_: e.activation(out=kT[:, h, :], in_=banks[qb_][:, 256:512],
                                                                 func=AF.Copy, scale=scl),
                     reads=[("psqk", h % 2)], writes=[("kT", h)], banks=[qb_])
                S.op("act", lambda e, h=h, qb_=qb_: e.activation(out=qT[:, h, :], in_=banks[qb_][:, 0:256],
                                                                 func=AF.Copy),
                     reads=[("psqk", h % 2)], writes=[("qT", h)], banks=[qb_])
                S.op("dve", lambda e, h=h, qb_=qb_: e.tensor_tensor(
                    out=qfT[:, h, :].rearrange("p (c t) -> p c t", c=2),
                    in0=banks[qb_][:, 0:256].rearrange("p (c t) -> p c t", c=2),
                    in1=Rf[:, h, :].unsqueeze(1).to_broadcast([128, 2, 128]), op=ALU.mult),
                    reads=[("psqk", h % 2), ("Rf", h)], writes=[("qfT", h)], banks=[qb_])
                S.op("dve", lambda e, h=h, qb_=qb_: e.tensor_tensor(
                    out=qbT[:, h, :].rearrange("p (c t) -> p c t", c=2),
                    in0=banks[qb_][:, 0:256].rearrange("p (c t) -> p c t", c=2),
                    in1=Rb[:, h, :].unsqueeze(1).to_broadcast([128, 2, 128]), op=ALU.mult),
                    reads=[("psqk", h % 2), ("Rb", h)], writes=[("qbT", h)], banks=[qb_])

        ksf = [kfs, A(197.5, 1024, BF16, "p (h t) -> p h t", h=4)]

        ksf_key = [("ks", 1), ("ksf", 1)]

        def k_scale_pool(n):
            S.op("pool", lambda e: e.tensor_tensor(
                out=ksf[n % 2], in0=k_tok[:, n, :].rearrange("p (h t) -> p h t", h=4),
                in1=kdec[:, 0:4].unsqueeze(2).to_broadcast([128, 4, 128]), op=ALU.mult),
                reads=["kdec"], writes=[ksf_key[n % 2]] + (["tD1", "tD2"] if n % 2 == 1 else []))

        def kvf_mm(n, hs):
            for h in hs:
                S.op("pe", lambda e, h=h: e.matmul(
                    banks[PS_KV][:, 256 * (h % 2):256 * (h % 2 + 1)], lhsT=ksf[n % 2][:, h, :],
                    rhs=v_tok[:, n, 256 * h:256 * (h + 1)], start=True, stop=True),
                    reads=[ksf_key[n % 2]], writes=[("pskv", PS_KV, h % 2)], banks=[PS_KV])

        def X(n):
            par = n % 2
            csl = slice(128 * (n % 2), 128 * (n % 2 + 1))
            S.op("act", lambda e: e.activation(out=Pfb[par], in_=Sst[:, 0].rearrange("p h e -> p (h e)"),
                                               func=AF.Copy), reads=[("S", 0)],
                 writes=[("Pf", par)] + (["ctab"] if par == 1 else []))
            if n + 1 < NOWN - 1:
                k_scale_pool(n + 1)
            if n < NOWN - 1:
                kvf_mm(n, (0, 1))
            for h in range(4):
                S.op("pe", lambda e, h=h: e.matmul(
                    banks[PS_SC][:, 128 * h:128 * (h + 1)], lhsT=kT[:, h, csl], rhs=qT[:, h, csl],
                    start=True, stop=True), reads=[("kT", h), ("qT", h)], writes=["pssc"], banks=[PS_SC])
            if n < NOWN - 1:
                s_update(0, (0, 1), lambda h: PS_KV)
                kvf_mm(n, (2, 3))
            S.op("dve", lambda e: e.tensor_tensor(
                out=ATb[par], in0=banks[PS_SC][:].rearrange("p (h t) -> p h t", h=4), in1=Dt, op=ALU.mult),
                reads=["pssc"] + Dkeys, writes=[("AT", par)] + ([("ks", 0)] if par == 1 else []),
                banks=[PS_SC])
            if n < NOWN - 1:
                s_update(0, (2, 3), lambda h: PS_KV)

        def Y(n):
            par = n % 2
            csl = slice(128 * (n % 2), 128 * (n % 2 + 1))
            for h in range(4):
                bank = PS_O[par][h // 2]
                osl = slice(256 * (h % 2), 256 * (h % 2 + 1))
                vsl = slice(256 * h, 256 * (h + 1))
                S.op("pe", lambda e, h=h, bank=bank, osl=osl, vsl=vsl: e.matmul(
                    banks[bank][:, osl], lhsT=ATb[par][:, h, :], rhs=v_tok[:, n, vsl], start=True, stop=False),
                    reads=[("AT", par)], writes=[("pso", par, h)], banks=[bank])
                S.op("pe", lambda e, h=h, bank=bank, osl=osl, vsl=vsl: e.matmul(
                    banks[bank][:, osl], lhsT=qfT[:, h, csl], rhs=Pfb[par][:, vsl], start=False, stop=False),
                    reads=[("qfT", h), ("Pf", par)], writes=[("pso", par, h)], banks=[bank])
                S.op("pe", lambda e, h=h, bank=bank, osl=osl, vsl=vsl: e.matmul(
                    banks[bank][:, osl], lhsT=qbT[:, h, csl], rhs=Pb[:, n, vsl], start=False, stop=True),
                    reads=[("qbT", h), ("Pb", n)], writes=[("pso", par, h)], banks=[bank])
            for h in range(4):
                bank = PS_O[par][h // 2]
                osl = slice(256 * (h % 2), 256 * (h % 2 + 1))
                S.op("dve", lambda e, h=h, bank=bank, osl=osl: e.bn_stats(out=st[:, h, :], in_=banks[bank][:, osl]),
                     reads=[("pso", par, h)], writes=[("st", h)], banks=[bank])
                S.op("dve", lambda e, h=h: e.bn_aggr(out=mv[:, h, :], in_=st[:, h, :]),
                     reads=[("st", h)], writes=[("mv", h)])
            mvk = [("mv", h) for h in range(4)]
            S.op("dve", lambda e: e.tensor_scalar(out=rsin, in0=mv[:, :, 1], scalar1=GN_EPS, scalar2=None,
                                                  op0=ALU.add), reads=mvk, writes=["rsin"])
            S.op("pool", lambda e: e.tensor_tensor(out=rs, in0=rsin, in1=cvec[:, 2:3].to_broadcast([128, 4]),
                                                   op=ALU.pow), reads=["rsin"], writes=["rs"])

        def Y2(n):
            par = n % 2
            mvk = [("mv", h) for h in range(4)]
            S.op("dve", lambda e: e.scalar_tensor_tensor(out=nmr, in0=mv[:, :, 0], scalar=-1.0, in1=rs,
                                                         op0=ALU.mult, op1=ALU.mult),
                 reads=mvk + ["rs"], writes=["nmr"])
            for h in range(4):
                bank = PS_O[par][h // 2]
                osl = slice(256 * (h % 2), 256 * (h % 2 + 1))
                S.op("act", lambda e, h=h, bank=bank, osl=osl: e.activation(
                    out=tn[:, 256 * h:256 * (h + 1)], in_=banks[bank][:, osl], func=AF.Identity,
                    scale=rs[:, h:h + 1], bias=nmr[:, h:h + 1]),
                    reads=[("pso", par, h), "rs", "nmr"], writes=[("tn", h)], banks=[bank])

        def Z(n):
            for k in range(8):
                S.op("pe", lambda e, k=k: e.transpose(out=tpz_bf[:, k, :], in_=tn[:, 128 * k:128 * (k + 1)],
                                                      identity=ident),
                     reads=[("tn", k // 2)], writes=["pstp2"], banks=[PS_TP])
            S.op("act", lambda e: e.activation(out=retnT[:, :, 128 * n:128 * (n + 1)], in_=tpz_bf, func=AF.Copy),
                 reads=["pstp2"], writes=[("retnT", n)], banks=[PS_TP])

        k_scale_pool(0)
        proj_block(0)
        X(0)
        for i in range(NOWN):
            Y(i)
            if i + 1 < NOWN:
                if (i + 1) % 2 == 0:
                    proj_block((i + 1) // 2)
                X(i + 1)
            if i >= 1:
                Z(i - 1)
            Y2(i)
        Z(NOWN - 1)
        if stage_limit == 2:
            dbg("retnT", retnT, [128, 8, 2048], BF16)
            dbg("Pb", Pb, [128, 16, 1024], BF16)
            dbg("Sfin", Sst, [128, 2, 4, 256], F32)

    if stage_limit >= 3:
        S.barrier()
        w_rg = A(80, 16384, BF16, "p (k c) -> p k c", k=8)
        w_ro = A(96, 16384, BF16, "p (k c) -> p k c", k=8)
        w_mr = A(112, 16384, BF16, "p (k c) -> p k c", k=8)
        sgb = [A(160 + j, 1024, BF16) for j in range(2)]
        gT = A(162, 8192, BF16, "p (k t) -> p k t", k=8)
        sigb = [A(170 + 2 * j, 2048, F32) for j in range(2)]
        S.dma("pool", lambda e: e.dma_start(out=w_rg, in_=win_v[:, :, C_RG:C_RG + 1024]), writes=["w_rg"])
        S.dma("pool", lambda e: e.dma_start(out=w_ro, in_=wro_v), writes=["w_ro"])
        S.dma("pool", lambda e: e.dma_start(out=w_mr, in_=win_v[:, :, C_MR:C_MR + 1024]), writes=["w_mr"])

        def gate_block(tb, w_g, srcT, srckey, gdst, wkey="w_rg"):
            tsl = slice(512 * tb, 512 * (tb + 1))
            for m in range(8):
                bank = m % 2
                for k in range(8):
                    S.op("pe", lambda e, k=k, m=m, bank=bank: e.matmul(
                        banks[bank][:], lhsT=w_g[:, k, 128 * m:128 * (m + 1)], rhs=hT_own[:, k, tsl],
                        start=(k == 0), stop=(k == 7)), reads=[wkey], writes=[("psg", bank)], banks=[bank])
                S.op("act", lambda e, m=m, bank=bank: e.activation(out=sgb[m % 2], in_=banks[bank][:], func=AF.Silu),
                     reads=[("psg", bank)], writes=[("sg", m % 2)], banks=[bank])
                S.op("dve", lambda e, m=m: e.tensor_tensor(out=gdst[:, m, :], in0=sgb[m % 2], in1=srcT[:, m, tsl],
                                                           op=ALU.mult),
                     reads=[("sg", m % 2), (srckey, m, tb)], writes=[("gT", m)])

        def s2b_block(tb):
            tsl = slice(512 * tb, 512 * (tb + 1))
            gate_block(tb, w_rg, retnT, "retnT_", gT)
            for j in range(8):
                by, bm = 2 + j % 2, 4 + j % 2
                for k in range(8):
                    S.op("pe", lambda e, k=k, j=j, bm=bm: e.matmul(
                        banks[bm][:], lhsT=w_mr[:, k, 128 * j:128 * (j + 1)], rhs=hT_own[:, k, tsl],
                        start=(k == 0), stop=(k == 7)), reads=["w_mr"], writes=[("psm", bm)], banks=[bm])
                for m in range(8):
                    S.op("pe", lambda e, m=m, j=j, by=by: e.matmul(
                        banks[by][:], lhsT=w_ro[:, m, 128 * j:128 * (j + 1)], rhs=gT[:, m, :],
                        start=(m == 0), stop=(m == 7)), reads=[("gT", m), "w_ro"], writes=[("psy", by)], banks=[by])
                S.op("act", lambda e, j=j, bm=bm: e.activation(out=sigb[j % 2], in_=banks[bm][:], func=AF.Sigmoid),
                     reads=[("psm", bm)], writes=[("sig", j % 2)], banks=[bm])
                S.op("dve", lambda e, j=j, by=by: e.tensor_tensor(out=mergedT[:, j, tsl], in0=banks[by][:],
                                                                   in1=sigb[j % 2], op=ALU.mult),
                     reads=[("psy", by), ("sig", j % 2)], writes=[("mergedT", j, tb)], banks=[by])

        for tb in range(4):
            s2b_block(tb)
        if stage_limit == 3:
            dbg("m1T", mergedT, [128, 8, 2048], BF16)

    if stage_limit >= 4:
        S.barrier()
        qT_att = A(80, 32768, BF16, "p (k t) -> p k t", k=8)
        w_aq = A(112, 16384, BF16, "p (k c) -> p k c", k=8)
        kT2 = A(128, 18432, BF16, "p (h t) -> p h t", h=4)
        w_akd = A(146, 8192, BF16, "p (k h c) -> p k h c", k=8, h=4)
        w_av = A(154, 4096, BF16, "p (k c) -> p k c", k=8)
        v_att = A(160, 18432, BF16, "p (e c) -> p e c", e=18)
        sq = A(178, 2048, F32)
        lnt = A(180, 2048, F32)
        rst = [A(182 + 2 * j, 2048, F32) for j in range(2)]
        S.dma("pool", lambda e: e.dma_start(out=w_aq, in_=win_v[:, :, C_AQ:C_AQ + 1024]), writes=["w_aq"])
        for hk_ in range(4):
            for u_ in range(2):
                S.dma("pool", lambda e, hk_=hk_, u_=u_: e.dma_start(
                    out=w_akd[:, :, hk_, 64 * u_:64 * (u_ + 1)],
                    in_=win_v[:, :, C_AK + 64 * hk_:C_AK + 64 * (hk_ + 1)]), writes=[("w_akd", hk_, u_)])
        S.dma("pool", lambda e: e.dma_start(out=w_av, in_=win_v[:, :, C_AV:C_AV + 256]), writes=["w_av"])
        cnt = [0]

        def ext_src(e):
            if e == 0:
                return hT_halo[:, :, 0:128]
            if e == 17:
                return hT_halo[:, :, 128:256]
            return hT_own[:, :, 128 * (e - 1):128 * e]

        sqb = [A(178, 1024, BF16), A(179, 1024, BF16)]
        oblk_bf = A(186, 256, BF16)
        S.op("dve", lambda e: e.tensor_copy(out=oblk_bf, in_=oblk), writes=["oblk_bf"])
        qk_items = []

        def qk_A(it, lhs_fn, rhs_ap, N, dst, dkeys, is_k, wkeys):
            ba = it % 3
            for k in range(8):
                S.op("pe", lambda e, k=k: e.matmul(banks[ba][:, 0:N], lhsT=lhs_fn(k), rhs=rhs_ap[:, k, :],
                                                   start=(k == 0), stop=(k == 7)),
                     reads=wkeys, writes=[("psa", ba)], banks=[ba])
            S.op("act", lambda e: e.activation(out=sqb[it % 2][:, 0:N], in_=banks[ba][:, 0:N], func=AF.Square),
                 reads=[("psa", ba)], writes=[("sq", it % 2)], banks=[ba])

        def qk_B(it, lhs_fn, rhs_ap, N, dst, dkeys, is_k, wkeys):
            ba, bs = it % 3, 3 + it % 2
            S.op("pe", lambda e: e.matmul(banks[bs][:, 0:N], lhsT=oblk_bf, rhs=sqb[it % 2][:, 0:N], start=True, stop=True),
                 reads=[("sq", it % 2), "oblk_bf"], writes=[("pss", bs)], banks=[bs])
            S.op("act", lambda e: e.activation(out=lnt[:, 0:N], in_=banks[bs][:, 0:N], func=AF.Ln,
                                               scale=1.0 / 64, bias=float(RMS_EPS)),
                 reads=[("pss", bs)], writes=["lnt"], banks=[bs])
            S.op("act", lambda e: e.activation(out=rst[it % 2][:, 0:N], in_=lnt[:, 0:N], func=AF.Exp, scale=-0.5),
                 reads=["lnt"], writes=[("rst", it % 2)])
            if is_k:
                S.op("dve", lambda e: e.scalar_tensor_tensor(out=dst, in0=banks[ba][:, 0:N], scalar=gkq[:, 0:1],
                                                             in1=rst[it % 2][:, 0:N], op0=ALU.mult, op1=ALU.mult),
                     reads=[("psa", ba), ("rst", it % 2)], writes=dkeys, banks=[ba])
            else:
                S.op("dve", lambda e: e.tensor_tensor(out=dst, in0=banks[ba][:, 0:N], in1=rst[it % 2][:, 0:N],
                                                      op=ALU.mult),
                     reads=[("psa", ba), ("rst", it % 2)], writes=dkeys, banks=[ba])

        def qknorm(*args):
            qk_items.append(args)

        epsb = A(46.125, 4, F32)
        S.op("dve", lambda e: e.memset(epsb, RMS_EPS), writes=["epsb"])
        for m in range(8):
            for tb in range(4):
                qknorm(lambda k, m=m: w_aq[:, k, 128 * m:128 * (m + 1)], hT_own[:, :, 512 * tb:512 * (tb + 1)], 512,
                       qT_att[:, m, 512 * tb:512 * (tb + 1)], [("qatt", m, qb) for qb in range(4 * tb, 4 * tb + 4)],
                       False, ["w_aq"])
        for hk in range(4):
            segs = [(hT_halo[:, :, 0:128], 128, 0), (hT_halo[:, :, 128:256], 128, 17 * 128)]
            segs += [(hT_own[:, :, 512 * tb:512 * (tb + 1)], 512, 128 + 512 * tb) for tb in range(4)]
            for (rhs_ap, N, t0) in segs:
                qknorm(lambda k, hk=hk: w_akd[:, k, hk, :], rhs_ap, N, kT2[:, hk, t0:t0 + N],
                       [("kT2", hk, (t0 + 128 * q) // 128) for q in range(N // 128)], True,
                       [("w_akd", hk, 0), ("w_akd", hk, 1)])
        qk_A(0, *qk_items[0])
        for it_ in range(len(qk_items)):
            if it_ + 1 < len(qk_items):
                qk_A(it_ + 1, *qk_items[it_ + 1])
            qk_B(it_, *qk_items[it_])
        for ex in range(18):
            bv = 5 + ex % 2
            srcx = ext_src(ex)
            for k in range(8):
                S.op("pe", lambda e, k=k, bv=bv, srcx=srcx: e.matmul(
                    banks[bv][:, 0:256], lhsT=srcx[:, k, :], rhs=w_av[:, k, :], start=(k == 0), stop=(k == 7)),
                    reads=["w_av"], writes=[("psv", bv)], banks=[bv])
            vview = v_att[:, ex, :].rearrange("p (h u c) -> p h u c", h=4, u=2)
            for u in range(1):
                if ex in (0, 17):
                    sc = halov[:, 0:1] if ex == 0 else halov[:, 1:2]
                    S.op("act", lambda e, u=u, bv=bv, vview=vview, sc=sc: e.activation(
                        out=vview[:, :, u, :], in_=banks[bv][:, 0:256].rearrange("p (h c) -> p h c", h=4),
                        func=AF.Copy, scale=sc), reads=[("psv", bv)], writes=[("v_att", ex, u)], banks=[bv])
                else:
                    S.op("act", lambda e, u=u, bv=bv, vview=vview: e.activation(
                        out=vview[:, :, u, :], in_=banks[bv][:, 0:256].rearrange("p (h c) -> p h c", h=4),
                        func=AF.Copy), reads=[("psv", bv)], writes=[("v_att", ex, u)], banks=[bv])
        if stage_limit == 4:
            dbg("qT_att", qT_att, [128, 8, 2048], BF16)
            dbg("kT2", kT2, [128, 4, 2304], BF16)
            dbg("v_att", v_att, [128, 18, 512], BF16)

    if stage_limit >= 5:
        S.barrier()
        Etab = A(180, 24576, F32, "p (h b t) -> p h b t", h=16, b=3)
        actab = A(112, 3072, F32, "p (n c) -> p n c", n=6)
        etmp = [A(115 + 3 * j, 3072, F32, "p (u c) -> p u c", u=2) for j in range(2)]
        PT = [A(121 + 1.5 * j, 1536, BF16, "p (b u t) -> p b u t", b=3, u=2) for j in range(2)]
        densb = [A(124 + j, 1024, F32) for j in range(2)]
        recsb = [A(126 + j, 1024, F32) for j in range(2)]
        etb = A(178, 512, F32)
        onef = A(46.1875, 4, F32)
        S.op("dve", lambda e: e.memset(onef, 1.0), writes=["onef"])
        S.dma("sp", lambda e: e.dma_start(
            out=actab, in_=const_d[:, cidx["ar0"] * 128:(cidx["ar0"] + 6) * 128].rearrange("p (n c) -> p n c", n=6)),
            writes=["actab"])
        for hq in range(16):
            slope = float(2.0 ** (-8.0 * (hq + 1.0) / 16))
            S.op("act", lambda e, hq=hq, slope=slope: e.activation(out=Etab[:, hq, :, :], in_=actab[:, 0:3, :],
                                                                   func=AF.Exp, scale=-slope,
                                                                   bias=negsink[:, hq:hq + 1]),
                 reads=["actab"], writes=[("Etab", hq)])
            S.op("dve", lambda e, hq=hq: e.tensor_tensor(out=Etab[:, hq, :, :], in0=Etab[:, hq, :, :],
                                                         in1=actab[:, 3:6, :], op=ALU.mult),
                 reads=[("Etab", hq), "actab"], writes=[("Etab", hq)])
        itc = [0]

        def att_A(it, qb, m):
            hk = m // 2
            se, so = 2 * (it % 2), 2 * (it % 2) + 1
            qsl = slice(128 * qb, 128 * (qb + 1))
            for kb in range(3):
                ex = qb + kb
                ksl = slice(128 * ex, 128 * (ex + 1))
                S.op("pe", lambda e, kb=kb, ksl=ksl: e.matmul(
                    banks[se][:, 128 * kb:128 * (kb + 1)], lhsT=kT2[0:64, hk, ksl], rhs=qT_att[0:64, m, qsl],
                    start=True, stop=True), reads=[("kT2", hk, ex), ("qatt", m, qb)], writes=[("pse", se)],
                    banks=[se])
                S.op("pe", lambda e, kb=kb, ksl=ksl: e.matmul(
                    banks[so][:, 128 * kb:128 * (kb + 1)], lhsT=kT2[64:128, hk, ksl], rhs=qT_att[64:128, m, qsl],
                    start=True, stop=True), reads=[("kT2", hk, ex), ("qatt", m, qb)], writes=[("pse", so)],
                    banks=[so])
            for u, bk in ((0, se), (1, so)):
                S.op("act", lambda e, u=u, bk=bk: e.activation(
                    out=etmp[it % 2][:, u, :], in_=banks[bk][:, 0:384], func=AF.Exp),
                    reads=[("pse", bk)], writes=[("etmp", it % 2, u)], banks=[bk])
                S.op("dve", lambda e, u=u: e.tensor_tensor(
                    out=PT[it % 2][:, :, u, :], in0=etmp[it % 2][:, u, :].rearrange("p (b t) -> p b t", b=3),
                    in1=Etab[:, 2 * m + u, :, :], op=ALU.mult),
                    reads=[("etmp", it % 2, u), ("Etab", 2 * m + u)], writes=[("PT", it % 2, u)])

        def att_B(it, qb, m):
            hk = m // 2
            bpv, bden = 4 + it % 2, 6 + it % 2
            qsl = slice(128 * qb, 128 * (qb + 1))
            for kb in range(3):
                ex = qb + kb
                ones_x = ones_bf
                if qb == 0 and kb == 0:
                    ones_x = ones_vb
                if qb == NOWN - 1 and kb == 2:
                    ones_x = ones_va
                for u in range(2):
                    psl = slice(64 * u, 64 * (u + 1))
                    S.op("pe", lambda e, kb=kb, ex=ex, u=u, psl=psl: e.matmul(
                        banks[bpv][psl, 0:128], lhsT=v_att[:, ex, 128 * hk:128 * hk + 64], rhs=PT[it % 2][:, kb, u, :],
                        start=(kb == 0), stop=(kb == 2)),
                        reads=[("PT", it % 2, u), ("v_att", ex, 0)], writes=[("pspv", bpv)], banks=[bpv])
                    S.op("pe", lambda e, kb=kb, u=u, psl=psl, ones_x=ones_x: e.matmul(
                        banks[bden][psl, 0:128], lhsT=ones_x[:, 0:64], rhs=PT[it % 2][:, kb, u, :],
                        start=(kb == 0), stop=(kb == 2)),
                        reads=[("PT", it % 2, u)], writes=[("psden", bden)], banks=[bden])
            S.op("act", lambda e: e.activation(out=densb[it % 2][:, 0:128], in_=banks[bden][:, 0:128], func=AF.Ln,
                                               bias=1.0),
                 reads=[("psden", bden)], writes=[("den", it % 2)], banks=[bden])
            S.op("act", lambda e: e.activation(out=recsb[it % 2][:, 0:128], in_=densb[it % 2][:, 0:128], func=AF.Exp,
                                               scale=-1.0),
                 reads=[("den", it % 2)], writes=[("rec", it % 2)])
            S.op("dve", lambda e: e.tensor_tensor(out=qT_att[:, m, qsl], in0=banks[bpv][:, 0:128],
                                                  in1=recsb[it % 2][:, 0:128], op=ALU.mult),
                 reads=[("pspv", bpv), ("rec", it % 2)], writes=[("qatt", m, qb)], banks=[bpv])

        its = [(qb, m) for qb in range(NOWN) for m in range(8)]
        att_A(0, *its[0])
        for it in range(len(its)):
            if it + 1 < len(its):
                att_A(it + 1, *its[it + 1])
            att_B(it, *its[it])
        if stage_limit == 5:
            dbg("attT", qT_att, [128, 8, 2048], BF16)
            dbg("m1T", mergedT, [128, 8, 2048], BF16)

    if stage_limit >= 6:
        S.barrier()
        w_ag = A(128, 16384, BF16, "p (k c) -> p k c", k=8)
        w_ao = A(144, 16384, BF16, "p (k c) -> p k c", k=8)
        w_ma = A(160, 16384, BF16, "p (k c) -> p k c", k=8)
        sgb3 = [A(176 + j, 1024, BF16) for j in range(2)]
        gT3 = A(178, 8192, BF16, "p (k t) -> p k t", k=8)
        sigb3 = [A(186 + 2 * j, 2048, F32) for j in range(2)]
        tmpm = [A(190 + 2 * j, 1024, BF16) for j in range(2)]
        S.dma("pool", lambda e: e.dma_start(out=w_ag, in_=win_v[:, :, C_AG:C_AG + 1024]), writes=["w_ag"])
        S.dma("pool", lambda e: e.dma_start(out=w_ao, in_=wao_v), writes=["w_ao"])
        S.dma("pool", lambda e: e.dma_start(out=w_ma, in_=win_v[:, :, C_MA:C_MA + 1024]), writes=["w_ma"])
        w_o = A(112, 16384, BF16, "p (k c) -> p k c", k=8)
        S.dma("pool", lambda e: e.dma_start(out=w_o, in_=wout_v), writes=["w_o_pre"])

        def s3c_block(tb):
            tsl = slice(512 * tb, 512 * (tb + 1))
            for m in range(8):
                bank = m % 2
                for k in range(8):
                    S.op("pe", lambda e, k=k, m=m, bank=bank: e.matmul(
                        banks[bank][:], lhsT=w_ag[:, k, 128 * m:128 * (m + 1)], rhs=hT_own[:, k, tsl],
                        start=(k == 0), stop=(k == 7)), reads=["w_ag"], writes=[("psg", bank)], banks=[bank])
                S.op("act", lambda e, m=m, bank=bank: e.activation(out=sgb3[m % 2], in_=banks[bank][:], func=AF.Silu),
                     reads=[("psg", bank)], writes=[("sg", m % 2)], banks=[bank])
                S.op("dve", lambda e, m=m: e.tensor_tensor(out=gT3[:, m, :], in0=sgb3[m % 2], in1=qT_att[:, m, tsl],
                                                           op=ALU.mult),
                     reads=[("sg", m % 2)], writes=[("gT3", m)])
            for j in range(8):
                by, bm = 2 + j % 2, 4 + j % 2
                for k in range(8):
                    S.op("pe", lambda e, k=k, j=j, bm=bm: e.matmul(
                        banks[bm][:], lhsT=w_ma[:, k, 128 * j:128 * (j + 1)], rhs=hT_own[:, k, tsl],
                        start=(k == 0), stop=(k == 7)), reads=["w_ma"], writes=[("psm", bm)], banks=[bm])
                for m in range(8):
                    S.op("pe", lambda e, m=m, j=j, by=by: e.matmul(
                        banks[by][:], lhsT=w_ao[:, m, 128 * j:128 * (j + 1)], rhs=gT3[:, m, :],
                        start=(m == 0), stop=(m == 7)), reads=[("gT3", m), "w_ao"], writes=[("psy", by)], banks=[by])
                S.op("act", lambda e, j=j, bm=bm: e.activation(out=sigb3[j % 2], in_=banks[bm][:], func=AF.Sigmoid),
                     reads=[("psm", bm)], writes=[("sig", j % 2)], banks=[bm])
                S.op("dve", lambda e, j=j, by=by: e.tensor_tensor(out=tmpm[j % 2], in0=banks[by][:],
                                                                   in1=sigb3[j % 2], op=ALU.mult),
                     reads=[("psy", by), ("sig", j % 2)], writes=[("tmpm", j % 2)], banks=[by])
                S.op("dve", lambda e, j=j: e.tensor_tensor(out=mergedT[:, j, tsl], in0=mergedT[:, j, tsl],
                                                           in1=tmpm[j % 2], op=ALU.add),
                     reads=[("tmpm", j % 2)], writes=[("mergedT", j, tb)])

        for tb in range(4):
            s3c_block(tb)
        if stage_limit == 6:
            dbg("mergedT", mergedT, [128, 8, 2048], BF16)

    if stage_limit >= 7:
        S.barrier()
        xt = [A(128 + 4 * j, 4096, F32) for j in range(NOWN)]
        ot = [A(80 + 4 * j, 4096, F32) for j in range(4)]
        for i_ in range(NOWN):
            S.dma("sp", lambda e, i_=i_: e.dma_start(out=xt[i_], in_=x_d[128 * i_:128 * (i_ + 1), :]),
                  writes=[("xt", i_)])

        def out_tile(i):
            for half in range(2):
                bank = (2 * i + half) % 8
                for j in range(8):
                    S.op("pe", lambda e, j=j, bank=bank, half=half: e.matmul(
                        banks[bank][:], lhsT=mergedT[:, j, 128 * i:128 * (i + 1)],
                        rhs=w_o[:, j, 512 * half:512 * (half + 1)], start=(j == 0), stop=(j == 7)),
                        reads=[], writes=[("pso5", bank)], banks=[bank])
                S.op("dve", lambda e, bank=bank, half=half: e.tensor_tensor(
                    out=ot[i % 4][:, 512 * half:512 * (half + 1)], in0=banks[bank][:],
                    in1=xt[i][:, 512 * half:512 * (half + 1)], op=ALU.add),
                    reads=[("pso5", bank), ("xt", i)], writes=[("ot", i % 4, half)], banks=[bank])
            S.dma("sp", lambda e: e.dma_start(out=out_d[128 * i:128 * (i + 1), :], in_=ot[i % 4]),
                  reads=[("ot", i % 4, 0), ("ot", i % 4, 1)], writes=[("out", i)])
            outs_written.append(("out", i))

        for i in range(NOWN):
            out_tile(i)

    S.barrier()
    fin_keys = []
    for (name, ap, t) in dbg_list:
        S.dma("sp", lambda e, ap=ap, t=t: e.dma_start(out=t, in_=ap), writes=[("dbgout", name)])
        fin_keys.append(("dbgout", name))
    if stage_limit < 7:
        S.dma("sp", lambda e: e.dma_start(out=out_d[0:128, :], in_=xbuf[0]), writes=[("out", 0)])
        fin_keys.append(("out", 0))
    S.op("sp", lambda e: e.nop(), reads=fin_keys + outs_written)
    S.finalize()
    semmap = {}
    for k in S.sem_keys():
        semmap[k] = es.enter_context(nc.semaphore("s_" + "_".join(str(t) for t in k)))
    with nc.Block() as block:
        @block.tensor
        def _(e):
            S.emit_engine("pe", e, semmap)

        @block.scalar
        def _(e):
            S.emit_engine("act", e, semmap)

        @block.vector
        def _(e):
            S.emit_engine("dve", e, semmap)

        @block.gpsimd
        def _(e):
            S.emit_engine("pool", e, semmap)

        @block.sync
        def _(e):
            S.emit_engine("sp", e, semmap)
    es.close()
    return nc, [n for (n, _, _) in dbg_list]


def make_in_maps(x, norm_g, w_in, ret_log_decay, q_norm_g, k_norm_g, attn_sink, w_ret_o, w_attn_o, w_out):
    x2 = np.ascontiguousarray(np.asarray(x, np.float32)[0])
    small = np.zeros((128, NSMALL), np.float32)
    small[:, SM_NG:SM_NG + 1024] = np.asarray(norm_g, np.float32).reshape(1, 1024)
    small[:, SM_LG:SM_LG + 8] = np.asarray(ret_log_decay, np.float32).reshape(1, 8)
    small[:, SM_SINK:SM_SINK + 16] = np.asarray(attn_sink, np.float32).reshape(1, 16)
    small[:, SM_GQ] = np.tile(np.asarray(q_norm_g, np.float32).reshape(64), 2)
    small[:, SM_GK] = np.tile(np.asarray(k_norm_g, np.float32).reshape(64), 2)
    w_in2 = np.ascontiguousarray(np.asarray(w_in, np.float32)[0])
    wro = np.ascontiguousarray(np.asarray(w_ret_o, np.float32)[0])
    wao = np.ascontiguousarray(np.asarray(w_attn_o, np.float32)[0])
    wout = np.ascontiguousarray(np.asarray(w_out, np.float32)[0])
    maps = []
    for c in range(NCORES):
        maps.append({
            "x": np.ascontiguousarray(np.roll(x2, -TOK * c, axis=0)),
            "w_in": w_in2, "w_ret_o": wro, "w_attn_o": wao, "w_out": wout,
            "small": small, "consts": CONST_ARR, "cvec": CONST_VEC, "cmask": _core_masks(c),
        })
    return maps


_NC_CACHE = {}


def kernel(x, norm_g, w_in, ret_log_decay, q_norm_g, k_norm_g, attn_sink, w_ret_o, w_attn_o, w_out):
    if "nc" not in _NC_CACHE:
        _NC_CACHE["nc"] = build()[0]
    nc = _NC_CACHE["nc"]
    maps = make_in_maps(x, norm_g, w_in, ret_log_decay, q_norm_g, k_norm_g, attn_sink, w_ret_o, w_attn_o, w_out)
    res = run_bass_kernel_spmd(nc, maps, core_ids=list(range(NCORES)))
    out = np.concatenate([np.asarray(res.results[c]["out"], np.float32) for c in range(NCORES)], axis=0)
    return out.reshape(1, SEQ, DM)
```

```python
import os
from contextlib import ExitStack
import numpy as np
import concourse.bass as bass
import concourse.mybir as mybir
from concourse.bass_utils import run_bass_kernel_spmd

F32 = mybir.dt.float32
BF16 = mybir.dt.bfloat16
U8 = mybir.dt.uint8
AF = mybir.ActivationFunctionType
ALU = mybir.AluOpType
AX = mybir.AxisListType

NCORES = 8
SEQ = 16384
DM = 1024
TOK = SEQ // NCORES
NT = SEQ // 128
NOWN = TOK // 128
D_IN = 7680
C_RQ, C_RK, C_RV, C_RG, C_AQ, C_AK, C_AV, C_AG, C_MR, C_MA = 0, 512, 1024, 2048, 3072, 4096, 4352, 4608, 5632, 6656
RMS_EPS = 1e-6
GN_EPS = 1e-6

ENG_NAMES = ("pe", "act", "dve", "pool", "sp")
SAME_ENGINE_KINDS = ("raw", "waw", "war")


class Op:
    __slots__ = ("eng", "fn", "deps", "inc_val", "need_inc", "is_dma", "dma_sem", "dma_val", "waits", "clock")

    def __init__(self, eng, fn, is_dma=False):
        self.eng = eng
        self.fn = fn
        self.deps = []
        self.inc_val = None
        self.need_inc = False
        self.is_dma = is_dma
        self.dma_sem = None
        self.dma_val = None


class Sched:
    def __init__(self, n_dma_sems=8):
        self.ops = {e: [] for e in ENG_NAMES}
        self.last_writer = {}
        self.readers = {}
        self.n_dma_sems = n_dma_sems
        self.dma_rr = {e: 0 for e in ENG_NAMES}
        self.dma_cnt = {}
        self.dma_last = {}
        self.bank_last = {}
        self.order = []

    def _track(self, op, reads, writes, banks=()):
        for b in banks:
            p = self.bank_last.get(b)
            if p is not None and p.eng != op.eng:
                op.deps.append((p, "bank"))
            self.bank_last[b] = op
        for k in reads:
            w = self.last_writer.get(k)
            if w is not None:
                op.deps.append((w, "raw"))
        for k in writes:
            w = self.last_writer.get(k)
            if w is not None:
                op.deps.append((w, "waw"))
            for r in self.readers.get(k, ()):
                if r is not op:
                    op.deps.append((r, "war"))
        for k in reads:
            self.readers.setdefault(k, []).append(op)
        for k in writes:
            self.last_writer[k] = op
            self.readers[k] = []

    def op(self, eng, fn, reads=(), writes=(), banks=()):
        o = Op(eng, fn)
        self._track(o, reads, writes, banks)
        self.ops[eng].append(o)
        self.order.append(o)
        return o

    def dma(self, eng, fn, reads=(), writes=()):
        o = Op(eng, fn, is_dma=True)
        self._track(o, reads, writes)
        slot = (eng, self.dma_rr[eng] % self.n_dma_sems)
        self.dma_rr[eng] += 1
        prev = self.dma_last.get(slot)
        if prev is not None:
            o.deps.append((prev, "raw"))
        self.dma_cnt[slot] = self.dma_cnt.get(slot, 0) + 1
        o.dma_sem = slot
        o.dma_val = 16 * self.dma_cnt[slot]
        self.dma_last[slot] = o
        self.ops[eng].append(o)
        self.order.append(o)
        return o

    def barrier(self):
        lasts = []
        for e in ENG_NAMES:
            for o in reversed(self.ops[e]):
                if not o.is_dma:
                    lasts.append(o)
                    break
        lasts += list(self.dma_last.values())
        for e in ENG_NAMES:
            o = Op(e, lambda h: h.nop())
            for p in lasts:
                o.deps.append((p, "raw"))
            self.ops[e].append(o)
            self.order.append(o)
        self.last_writer = {}
        self.readers = {}
        self.bank_last = {}

    def finalize(self):
        for e in ENG_NAMES:
            for o in self.ops[e]:
                for (p, kind) in o.deps:
                    if p.is_dma:
                        continue
                    if p.eng == o.eng and (p.eng == "pe" or kind not in SAME_ENGINE_KINDS):
                        continue
                    p.need_inc = True
        for e in ENG_NAMES:
            c = 0
            for o in self.ops[e]:
                if o.is_dma:
                    continue
                if o.need_inc:
                    c += 1
                    o.inc_val = c

        known = {e: {} for e in ENG_NAMES}
        for o in self.order:
            kn = known[o.eng]
            need = {}
            for (p, kind) in o.deps:
                if p.is_dma:
                    k, v = p.dma_sem, p.dma_val
                else:
                    if p.eng == o.eng and (p.eng == "pe" or kind not in SAME_ENGINE_KINDS):
                        continue
                    k, v = ("eng", p.eng), p.inc_val
                if v > need.get(k, (0, None))[0]:
                    need[k] = (v, p)
            waits = []
            for k, (v, p) in need.items():
                if kn.get(k, 0) >= v:
                    continue
                waits.append((k, v))
                for kk, vv in p.clock.items():
                    if vv > kn.get(kk, 0):
                        kn[kk] = vv
            o.waits = waits
            clk = dict(kn)
            if o.is_dma:
                clk[o.dma_sem] = max(clk.get(o.dma_sem, 0), o.dma_val)
            elif o.need_inc:
                clk[("eng", o.eng)] = max(clk.get(("eng", o.eng), 0), o.inc_val)
            o.clock = clk

    def sem_keys(self):
        return [("eng", e) for e in ENG_NAMES] + sorted(self.dma_cnt.keys())

    def emit_engine(self, e, handle, semmap):
        for o in self.ops[e]:
            for (k, v) in o.waits:
                handle.wait_ge(semmap[k], v)
            ins = o.fn(handle)
            if o.is_dma:
                ins.then_inc(semmap[o.dma_sem], 16)
            elif o.need_inc:
                ins.then_inc(semmap[("eng", e)], 1)


def _const_tables():
    p = np.arange(128, dtype=np.float64)[:, None]
    i = np.arange(128, dtype=np.float64)[None, :]
    t = {}
    t["posf"] = SEQ - 1 - (128 * i + p)
    t["posb"] = (128 * i + p) - TOK
    t["ip1"] = np.broadcast_to(i + 1, (128, 128))
    t["cmi"] = np.broadcast_to(128 - i, (128, 128))
    t["dpos"] = np.maximum(i - p, 0)
    t["dneg"] = np.maximum(p - i, 0)
    t["mge"] = (i >= p).astype(np.float64)
    t["mlt"] = (p > i).astype(np.float64)
    t["ar0"] = i + 128 - p
    t["ar1"] = np.abs(p - i)
    t["ar2"] = p + 128 - i
    t["am0"] = (p >= i).astype(np.float64)
    t["am1"] = np.ones((128, 128))
    t["am2"] = (p <= i).astype(np.float64)
    t["ident"] = np.eye(128)
    t["oblk"] = ((p // 64) == (i // 64)).astype(np.float64)
    t["ones"] = np.ones((128, 128))
    names = list(t.keys())
    arr = np.concatenate([np.asarray(t[n], dtype=np.float32) for n in names], axis=1)
    vec = np.zeros((128, 8), np.float32)
    vec[:, 0] = 127 - np.arange(128)
    vec[:, 1] = np.arange(128)
    vec[:, 2] = -0.5
    return names, np.ascontiguousarray(arr), vec


CONST_NAMES, CONST_ARR, CONST_VEC = _const_tables()
NCONST = CONST_ARR.shape[1]


def _core_masks(c):
    m = np.zeros((128, 128 + 128 + 8 + 8 + 2), np.float32)
    for tile in range(NOWN, NT):
        g = tile // NOWN
        after = (c + g) < NCORES
        m[:, tile] = 0.0 if after else 1.0
        m[:, 128 + tile] = 1.0 if after else 0.0
    for g in range(1, 8):
        after = (c + g) < NCORES
        m[:, 256 + g] = 0.0 if after else 1.0
        m[:, 264 + g] = 1.0 if after else 0.0
    m[:, 272] = 1.0 if c > 0 else 0.0
    m[:, 273] = 1.0 if c < NCORES - 1 else 0.0
    return m


SM_NG, SM_LG, SM_SINK, SM_GQ, SM_GK = 0, 1024, 1032, 1048, 1049
NSMALL = 1050


def build(stage_limit=99, debug=False):
    nc = bass.Bass("TRN2", target_bir_lowering=False)
    x_d = nc.dram_tensor("x", [SEQ, DM], F32, kind="ExternalInput").ap()
    win_d = nc.dram_tensor("w_in", [DM, D_IN], F32, kind="ExternalInput").ap()
    wro_d = nc.dram_tensor("w_ret_o", [DM, DM], F32, kind="ExternalInput").ap()
    wao_d = nc.dram_tensor("w_attn_o", [DM, DM], F32, kind="ExternalInput").ap()
    wout_d = nc.dram_tensor("w_out", [DM, DM], F32, kind="ExternalInput").ap()
    small_d = nc.dram_tensor("small", [128, NSMALL], F32, kind="ExternalInput").ap()
    const_d = nc.dram_tensor("consts", [128, NCONST], F32, kind="ExternalInput").ap()
    cvec_d = nc.dram_tensor("cvec", [128, 8], F32, kind="ExternalInput").ap()
    cmask_d = nc.dram_tensor("cmask", [128, 274], F32, kind="ExternalInput").ap()
    out_d = nc.dram_tensor("out", [TOK, DM], F32, kind="ExternalOutput").ap()
    dbg_list = []

    win_v = win_d.rearrange("(k p) e -> p k e", p=128)
    wro_v = wro_d.rearrange("(k p) e -> p k e", p=128)
    wao_v = wao_d.rearrange("(k p) e -> p k e", p=128)
    wout_v = wout_d.rearrange("(k p) e -> p k e", p=128)

    es = ExitStack()
    ARENA = 206 * 1024
    arena = es.enter_context(nc.sbuf_tensor("arena", [128, ARENA], U8))
    ps_all = es.enter_context(nc.psum_tensor("ps_all", [128, 4096], F32))
    banks = [ps_all[:, 512 * b:512 * (b + 1)] for b in range(8)]

    def A(off_kb, nbytes, dt, pat=None, **kw):
        off = int(off_kb * 1024)
        v = arena[:, off:off + nbytes].bitcast(dt)
        if pat is not None:
            v = v.rearrange(pat, **kw)
        return v

    S = Sched(n_dma_sems=8)

    hT_own = A(0, 32768, BF16, "p (k t) -> p k t", k=8)
    hT_halo = A(32, 4096, BF16, "p (k t) -> p k t", k=8)
    Sst = A(36, 8192, F32, "p (d h e) -> p d h e", d=2, h=4)
    ident = A(44, 256, BF16)
    ones_bf = A(44.25, 256, BF16)
    oblk = A(44.5, 512, F32)
    lgrep = A(45, 32, F32)
    gC = A(45.03125, 32, F32)
    kdec = A(45.0625, 32, F32)
    cvec = A(45.09375, 32, F32)
    negsink = A(45.125, 64, F32)
    gkq = A(45.1875, 4, F32)
    halov = A(45.25, 8, F32)
    grpm = A(45.3125, 64, F32)
    ones_vb = A(45.5, 256, BF16)
    ones_va = A(45.75, 256, BF16)
    gq_t = A(46, 4, F32)
    gk_t = A(46.0625, 4, F32)
    mergedT = A(48, 32768, BF16, "p (k t) -> p k t", k=8)

    cidx = {n: j for j, n in enumerate(CONST_NAMES)}

    def dbg(name, ap, shape, dt=F32):
        if not debug:
            return
        t = nc.dram_tensor("dbg_" + name, list(shape), dt, kind="ExternalOutput").ap()
        dbg_list.append(("dbg_" + name, ap, t))

    k_tok = A(80, 16384, BF16, "p (n c) -> p n c", n=NOWN)
    v_tok = A(96, 32768, BF16, "p (n c) -> p n c", n=NOWN)
    w_kv = A(128, 24576, BF16, "p (k c) -> p k c", k=8)
    xbuf = [A(160 + 4 * j, 4096, F32) for j in range(3)]
    sqs = A(172, 2048, BF16)
    NHN = 5
    hn = [A(174 + 2 * j, 2048, BF16) for j in range(NHN)]
    hT_r = [A(184 + 2 * j, 2048, BF16, "p (k t) -> p k t", k=8) for j in range(2)]
    k_sb = [A(188 + 0.5 * j, 512, BF16) for j in range(2)]
    kw = [A(189 + 0.5 * j, 512, BF16) for j in range(2)]
    v_dec = [A(190 + j, 1024, BF16) for j in range(2)]
    W_all = A(192, 2048, F32, "p (t h) -> p t h", h=4)
    g_tile = A(194, 4096, F32)
    cmask = A(158.5, 274 * 4, F32)
    ss = A(198, 16, F32)
    ms = A(198.0625, 16, F32)
    rstd = A(198.125, 16, F32)
    Jb = A(156.5, 2048, BF16)
    JT = A(152, 2048, BF16, "p (k t) -> p k t", k=8)
    small_sb = A(152, NSMALL * 4, F32)
    E1 = A(203, 2048, F32, "p (t h) -> p t h", h=4)
    ctab = A(199, 8 * 512, F32, "p (n c) -> p n c", n=8)
    E2 = A(156.5, 2048, F32, "p (t h) -> p t h", h=4)

    S.dma("sp", lambda e: e.dma_start(out=small_sb, in_=small_d), writes=["small"])
    S.dma("sp", lambda e: e.dma_start(out=ctab, in_=const_d[:, 0:1024].rearrange("p (n c) -> p n c", n=8)),
          writes=["ctab"])
    S.dma("sp", lambda e: e.dma_start(out=cmask, in_=cmask_d), writes=["cmask"])
    S.dma("sp", lambda e: e.dma_start(out=cvec, in_=cvec_d), writes=["cvec"])
    S.dma("sp", lambda e: e.dma_start(
        out=oblk, in_=const_d[:, cidx["oblk"] * 128:(cidx["oblk"] + 1) * 128]), writes=["oblk"])
    S.dma("pool", lambda e: e.dma_start(
        out=ident, in_=const_d[:, cidx["ident"] * 128:(cidx["ident"] + 1) * 128]), writes=["ident"])
    S.dma("pool", lambda e: e.dma_start(
        out=ones_bf, in_=const_d[:, cidx["ones"] * 128:(cidx["ones"] + 1) * 128]), writes=["ones_bf"])
    S.dma("pool", lambda e: e.dma_start(out=w_kv[:, :, 0:512], in_=win_v[:, :, C_RK:C_RK + 512]),
          writes=[("w_kv", 0)])
    S.dma("pool", lambda e: e.dma_start(out=w_kv[:, :, 512:1536], in_=win_v[:, :, C_RV:C_RV + 1024]),
          writes=[("w_kv", 1)])

    S.op("dve", lambda e: e.tensor_copy(out=g_tile, in_=small_sb[:, SM_NG:SM_NG + 1024]),
         reads=["small"], writes=["g_tile"])
    S.op("dve", lambda e: e.tensor_copy(out=lgrep, in_=small_sb[:, SM_LG:SM_LG + 8]),
         reads=["small"], writes=["lgrep"])
    S.op("dve", lambda e: e.tensor_scalar(out=negsink, in0=small_sb[:, SM_SINK:SM_SINK + 16], scalar1=-1.0,
                                          scalar2=None, op0=ALU.mult), reads=["small"], writes=["negsink"])
    S.op("dve", lambda e: e.scalar_tensor_tensor(out=gkq, in0=small_sb[:, SM_GQ:SM_GQ + 1], scalar=0.125,
                                                 in1=small_sb[:, SM_GK:SM_GK + 1], op0=ALU.mult, op1=ALU.mult),
         reads=["small"], writes=["gkq"])
    S.op("dve", lambda e: e.tensor_copy(out=halov, in_=cmask[:, 272:274]), reads=["cmask"], writes=["halov"])
    S.op("dve", lambda e: e.tensor_copy(out=grpm, in_=cmask[:, 256:272]), reads=["cmask"], writes=["grpm"])
    S.op("dve", lambda e: e.tensor_scalar(out=ones_vb, in0=ones_bf, scalar1=halov[:, 0:1], scalar2=None,
                                          op0=ALU.mult), reads=["ones_bf", "halov"], writes=["ones_vb"])
    S.op("dve", lambda e: e.tensor_scalar(out=ones_va, in0=ones_bf, scalar1=halov[:, 1:2], scalar2=None,
                                          op0=ALU.mult), reads=["ones_bf", "halov"], writes=["ones_va"])
    S.op("dve", lambda e: e.memset(Sst, 0.0), writes=["Sst"])
    S.op("act", lambda e: e.activation(out=gC, in_=lgrep, func=AF.Exp, scale=128.0),
         reads=["lgrep"], writes=["gC"])
    S.op("act", lambda e: e.activation(out=kdec[:, 0:4], in_=lgrep[:, 0:4], func=AF.Exp, scale=cvec[:, 0:1]),
         reads=["lgrep", "cvec"], writes=["kdec"])
    S.op("act", lambda e: e.activation(out=kdec[:, 4:8], in_=lgrep[:, 4:8], func=AF.Exp, scale=cvec[:, 1:2]),
         reads=["lgrep", "cvec"], writes=["kdec"])
    for h in range(4):
        S.op("act", lambda e, h=h: e.activation(out=E1[:, NOWN:NT, h], in_=ctab[:, 0, NOWN:NT], func=AF.Exp,
                                                scale=lgrep[:, h:h + 1]),
             reads=["ctab", "lgrep"], writes=[("E1", h)])
        S.op("act", lambda e, h=h: e.activation(out=E2[:, NOWN:NT, h], in_=ctab[:, 1, NOWN:NT], func=AF.Exp,
                                                scale=lgrep[:, 4 + h:5 + h]),
             reads=["ctab", "lgrep"], writes=[("E2", h)])
    rk = [("E1", h) for h in range(4)] + [("E2", h) for h in range(4)]
    S.op("dve", lambda e: e.tensor_tensor(
        out=E1[:, NOWN:NT, :], in0=E1[:, NOWN:NT, :],
        in1=cmask[:, NOWN:NT].unsqueeze(2).to_broadcast([128, NT - NOWN, 4]), op=ALU.mult),
        reads=rk + ["cmask"], writes=["E1m"])
    S.op("dve", lambda e: e.tensor_tensor(
        out=E2[:, NOWN:NT, :], in0=E2[:, NOWN:NT, :],
        in1=cmask[:, 128 + NOWN:128 + NT].unsqueeze(2).to_broadcast([128, NT - NOWN, 4]), op=ALU.mult),
        reads=rk + ["cmask"], writes=["E2m"])
    S.op("dve", lambda e: e.tensor_tensor(out=W_all[:, NOWN:NT, :], in0=E1[:, NOWN:NT, :],
                                          in1=E2[:, NOWN:NT, :], op=ALU.add),
         reads=["E1m", "E2m"], writes=["W_all"])

    S.op("dve", lambda e: e.tensor_scalar(out=W_all[:, NOWN:NT, 0:2], in0=W_all[:, NOWN:NT, 0:2],
                                          scalar1=float(128 ** -0.5), scalar2=None, op0=ALU.mult),
         reads=["W_all"], writes=["W_all"])
    PS_TP, PS_K, PS_V0, PS_V1 = 0, 1, 2, 3
    PS_T = 3
    PS_J = [[4, 5], [6, 7]]
    tp_bf = banks[PS_TP][:].bitcast(BF16).rearrange("p (k t) -> p k t", k=8)
    order = list(range(NT))
    if stage_limit == 0:
        order = order[:int(os.environ.get("K_NT0", "20"))]
    grp_cnt = {g: 0 for g in range(1, 8)}
    grp_tot = {g: sum(1 for t in order if t >= NOWN and t // NOWN == g) for g in range(1, 8)}
    run_first = [True]

    def front(s):
        i = order[s]
        xb = xbuf[s % 3]
        S.dma("sp", lambda e: e.dma_start(out=xb, in_=x_d[128 * i:128 * (i + 1), :]), writes=[("x", s % 3)])
        S.op("act", lambda e: e.activation(out=sqs, in_=xb, func=AF.Square, accum_out=ss[:, s % 4:s % 4 + 1]),
             reads=[("x", s % 3)], writes=["sqs", ("ss", s % 4)])
        S.op("dve", lambda e: e.tensor_scalar(out=ms[:, s % 4:s % 4 + 1], in0=ss[:, s % 4:s % 4 + 1],
                                              scalar1=1.0 / DM, scalar2=RMS_EPS, op0=ALU.mult, op1=ALU.add),
             reads=[("ss", s % 4)], writes=[("ms", s % 4)])
        S.op("pool", lambda e: e.tensor_tensor(out=rstd[:, s % 4:s % 4 + 1], in0=ms[:, s % 4:s % 4 + 1],
                                               in1=cvec[:, 2:3], op=ALU.pow),
             reads=[("ms", s % 4), "cvec"], writes=[("rstd", s % 4)])
        S.op("dve", lambda e: e.scalar_tensor_tensor(out=hn[s % NHN], in0=xb, scalar=rstd[:, s % 4:s % 4 + 1],
                                                     in1=g_tile, op0=ALU.mult, op1=ALU.mult),
             reads=[("x", s % 3), ("rstd", s % 4), "g_tile"], writes=[("hn", s % NHN)])

    def hT_dst(s):
        i = order[s]
        if i < NOWN:
            return hT_own[:, :, 128 * i:128 * (i + 1)], [("hT_own", i)]
        return hT_r[s % 2], [("hT_r", s % 2)]

    def mid(s):
        i = order[s]
        for k in range(8):
            S.op("pe", lambda e, k=k: e.transpose(out=tp_bf[:, k, :], in_=hn[s % NHN][:, 128 * k:128 * (k + 1)],
                                                  identity=ident),
                 reads=[("hn", s % NHN), "ident"], writes=[("ps", PS_TP)])
        dst, keys = hT_dst(s)
        S.op("act", lambda e: e.activation(out=dst, in_=tp_bf, func=AF.Copy), reads=[("ps", PS_TP)], writes=keys)
        if i == NOWN:
            S.op("pool", lambda e: e.tensor_copy(out=hT_halo[:, :, 128:256], in_=dst), reads=keys,
                 writes=[("hT_halo", 1)])
        if i == NT - 1:
            S.op("pool", lambda e: e.tensor_copy(out=hT_halo[:, :, 0:128], in_=dst), reads=keys,
                 writes=[("hT_halo", 0)])

    def proj(s):
        i = order[s]
        src, keys = hT_dst(s)
        scl0 = float(128 ** -0.5)
        if i < NOWN:
            plan = ((PS_K, 0), (PS_V0, 512), (PS_V1, 1024))
        else:
            plan = ((PS_K, 0), (PS_V0, 1024))
        for (bank, c0) in plan:
            for k in range(8):
                S.op("pe", lambda e, k=k, bank=bank, c0=c0: e.matmul(
                    banks[bank][:], lhsT=src[:, k, :], rhs=w_kv[:, k, c0:c0 + 512], start=(k == 0), stop=(k == 7)),
                    reads=keys + [("w_kv", 0 if c0 == 0 else 1)], writes=[("ps", bank)])
        if i < NOWN:
            S.op("act", lambda e: e.activation(out=k_tok[:, i, :], in_=banks[PS_K][:], func=AF.Copy, scale=scl0),
                 reads=[("ps", PS_K)], writes=[("k_tok", i)])
            S.op("act", lambda e: e.activation(out=v_tok[:, i, 0:512], in_=banks[PS_V0][:], func=AF.Copy),
                 reads=[("ps", PS_V0)], writes=[("v_tok", i, 0)])
            S.op("dve", lambda e: e.tensor_copy(out=v_tok[:, i, 512:1024], in_=banks[PS_V1][:]),
                 reads=[("ps", PS_V1)], writes=[("v_tok", i, 1)])
        else:
            S.op("act", lambda e: e.activation(out=k_sb[s % 2], in_=banks[PS_K][:, 256:512], func=AF.Copy, scale=scl0),
                 reads=[("ps", PS_K)], writes=[("k_sb", s % 2)])
            for h in range(2):
                S.op("act", lambda e, h=h: e.activation(
                    out=kw[s % 2][:, 128 * h:128 * (h + 1)], in_=banks[PS_K][:, 128 * h:128 * (h + 1)], func=AF.Copy,
                    scale=W_all[:, i, h:h + 1]), reads=[("ps", PS_K), "W_all"], writes=[("kw", s % 2)])
            S.op("dve", lambda e: e.tensor_tensor(
                out=v_dec[s % 2].rearrange("p (h e) -> p h e", h=2),
                in0=banks[PS_V0][:].rearrange("p (h e) -> p h e", h=2),
                in1=W_all[:, i, 2:4].unsqueeze(2).to_broadcast([128, 2, 256]), op=ALU.mult),
                reads=[("ps", PS_V0), "W_all"], writes=[("v_dec", s % 2)])

    NS = len(order)
    run_end = {}
    for s in range(NS):
        i = order[s]
        if i >= NOWN:
            g = i // NOWN
            nxt = order[s + 1] if s + 1 < NS else None
            run_end[s] = (nxt is None) or (nxt < NOWN) or (nxt // NOWN != g)

    def kvacc2(s):
        i = order[s]
        if i < NOWN:
            return
        g = i // NOWN
        st = run_first[0]
        run_first[0] = False
        for h in range(2):
            S.op("pe", lambda e, h=h, st=st: e.matmul(
                banks[PS_T][:, 256 * h:256 * (h + 1)], lhsT=k_sb[s % 2][:, 128 * h:128 * (h + 1)],
                rhs=v_dec[s % 2][:, 256 * h:256 * (h + 1)], start=(st and h == 0), stop=bool(run_end[s]),
                skip_group_check=True),
                reads=[("k_sb", s % 2), ("v_dec", s % 2)], writes=[("ps", PS_T)])
        for h in range(2):
            for half in range(2):
                bank = PS_J[h][half]
                S.op("pe", lambda e, h=h, half=half, bank=bank, st=st: e.matmul(
                    banks[bank][:], lhsT=kw[s % 2][:, 128 * h:128 * (h + 1)],
                    rhs=hn[s % NHN][:, 512 * half:512 * (half + 1)], start=st, stop=bool(run_end[s])),
                    reads=[("kw", s % 2), ("hn", s % NHN)], writes=[("ps", bank)])
        if run_end[s]:
            for d in range(2):
                S.op("dve", lambda e, d=d, g=g: e.scalar_tensor_tensor(
                    out=Sst[:, d, 2:4, :], in0=banks[PS_T][:].rearrange("p (h e) -> p h e", h=2),
                    scalar=grpm[:, 8 * d + g:8 * d + g + 1], in1=Sst[:, d, 2:4, :],
                    op0=ALU.mult, op1=ALU.add),
                    reads=[("ps", PS_T), "grpm", "Sst"], writes=["Sst"])
            for h in range(2):
                for half in range(2):
                    bank = PS_J[h][half]
                    S.op("act", lambda e, half=half, bank=bank: e.activation(
                        out=Jb[:, 512 * half:512 * (half + 1)], in_=banks[bank][:], func=AF.Copy),
                        reads=[("ps", bank), "W_all"], writes=[("Jb", half)])
                for k in range(8):
                    S.op("pe", lambda e, k=k: e.transpose(out=tp_bf[:, k, :], in_=Jb[:, 128 * k:128 * (k + 1)],
                                                          identity=ident),
                         reads=[("Jb", k // 4), "ident"], writes=[("ps", PS_TP)])
                S.op("act", lambda e: e.activation(out=JT, in_=tp_bf, func=AF.Copy),
                     reads=[("ps", PS_TP), "g_tile", "lgrep", "negsink", "gkq"],
                     writes=["JT"])
                for k in range(8):
                    S.op("pe", lambda e, k=k, h=h: e.matmul(
                        banks[PS_V0][:, 0:256], lhsT=JT[:, k, :], rhs=w_kv[:, k, 512 + 256 * h:512 + 256 * (h + 1)],
                        start=(k == 0), stop=(k == 7)), reads=["JT", ("w_kv", 1)], writes=[("ps", PS_V0)])
                for d in range(2):
                    S.op("dve", lambda e, d=d, g=g, h=h: e.scalar_tensor_tensor(
                        out=Sst[:, d, h, :], in0=banks[PS_V0][:, 0:256],
                        scalar=grpm[:, 8 * d + g:8 * d + g + 1], in1=Sst[:, d, h, :],
                        op0=ALU.mult, op1=ALU.add),
                        reads=[("ps", PS_V0), "grpm", "Sst"], writes=["Sst"])
            run_first[0] = True

    for s in range(NS + 3):
        if s < NS:
            front(s)
        if 1 <= s and s - 1 < NS:
            mid(s - 1)
        if 2 <= s and s - 2 < NS:
            proj(s - 2)
        if 3 <= s and s - 3 < NS:
            kvacc2(s - 3)

    if stage_limit == 0:
        dbg("hT_own", hT_own[:, :, 0:256], [128, 8, 256], BF16)
        dbg("hT_halo", hT_halo, [128, 8, 256], BF16)
        dbg("k_tok", k_tok[:, 0:2, :], [128, 2, 512], BF16)
        dbg("v_tok", v_tok[:, 0:2, :], [128, 2, 1024], BF16)
        dbg("Sst", Sst, [128, 2, 4, 256], F32)
        dbg("W_all", W_all, [128, 128, 4], F32)
    outs_written = []

    if stage_limit >= 2:
        S.barrier()
        Pb = A(48, 32768, BF16, "p (n c) -> p n c", n=NOWN)
        retnT = A(128, 32768, BF16, "p (k t) -> p k t", k=8)
        w_qk = A(160, 16384, BF16, "p (k c) -> p k c", k=8)
        qT = A(176, 2048, BF16, "p (h t) -> p h t", h=4)
        qfT = A(178, 2048, BF16, "p (h t) -> p h t", h=4)
        qbT = A(180, 2048, BF16, "p (h t) -> p h t", h=4)
        kT = A(182, 2048, BF16, "p (h t) -> p h t", h=4)
        Dt = A(184, 2048, F32, "p (h t) -> p h t", h=4)
        Rf = A(186, 2048, F32, "p (h t) -> p h t", h=4)
        Rb = A(188, 2048, F32, "p (h t) -> p h t", h=4)
        AT = A(190, 1024, BF16, "p (h t) -> p h t", h=4)
        kfs = A(191, 1024, BF16, "p (h t) -> p h t", h=4)
        kbs = A(192, 1024, BF16, "p (h t) -> p h t", h=4)
        Pf_bf = A(193, 2048, BF16)
        tn = A(195, 2048, BF16)
        st = A(197, 96, F32, "p (h s) -> p h s", h=4)
        mv = A(197.125, 32, F32, "p (h s) -> p h s", h=4)
        rsin = A(197.1875, 16, F32)
        rs = A(197.25, 16, F32)
        nmr = A(197.3125, 16, F32)
        tD1 = A(197.5, 512, F32)
        tD2 = A(198, 512, F32)
        PS_QA, PS_QB, PS_SCTP, PS_KV = 0, 1, 2, 3
        PS_O = [[4, 5], [6, 7]]
        tp2_bf = banks[PS_SCTP][:].bitcast(BF16).rearrange("p (k t) -> p k t", k=8)
        S.dma("pool", lambda e: e.dma_start(out=w_qk, in_=win_v[:, :, 0:1024]), writes=["w_qk"])
        for h in range(4):
            S.op("act", lambda e, h=h: e.activation(out=Rf[:, h, :], in_=ctab[:, 2, :], func=AF.Exp,
                                                    scale=lgrep[:, h:h + 1]), reads=["ctab"], writes=[("Rf", h)])
            S.op("act", lambda e, h=h: e.activation(out=Rb[:, h, :], in_=ctab[:, 3, :], func=AF.Exp,
                                                    scale=lgrep[:, 4 + h:5 + h]), reads=["ctab"], writes=[("Rb", h)])
            S.op("act", lambda e, h=h: e.activation(out=tD1, in_=ctab[:, 4, :], func=AF.Exp,
                                                    scale=lgrep[:, h:h + 1]), reads=["ctab"], writes=["tD1"])
            S.op("act", lambda e, h=h: e.activation(out=tD2, in_=ctab[:, 5, :], func=AF.Exp,
                                                    scale=lgrep[:, 4 + h:5 + h]), reads=["ctab"], writes=["tD2"])
            S.op("dve", lambda e: e.tensor_tensor(out=tD1, in0=tD1, in1=ctab[:, 6, :], op=ALU.mult),
                 reads=["tD1", "ctab"], writes=["tD1"])
            S.op("dve", lambda e: e.tensor_tensor(out=tD2, in0=tD2, in1=ctab[:, 7, :], op=ALU.mult),
                 reads=["tD2", "ctab"], writes=["tD2"])
            S.op("dve", lambda e, h=h: e.tensor_tensor(out=Dt[:, h, :], in0=tD1, in1=tD2, op=ALU.add),
                 reads=["tD1", "tD2"], writes=[("Dt", h)])
        Dkeys = [("Dt", h) for h in range(4)]
        ksb = [kbs, kfs]
        ATb = [AT, kbs]
        Pfb = [Pf_bf, A(199, 2048, BF16)]

        def k_scale(n, d, j):
            S.op("pool", lambda e: e.tensor_tensor(
                out=ksb[j], in0=k_tok[:, n, :].rearrange("p (h t) -> p h t", h=4),
                in1=kdec[:, 4 * d:4 * d + 4].unsqueeze(2).to_broadcast([128, 4, 128]), op=ALU.mult),
                reads=["kdec"], writes=[("ks", j)])

        def kv_mm(n, j, hs, bank_of):
            for h in hs:
                bank = bank_of(h)
                S.op("pe", lambda e, h=h, bank=bank: e.matmul(
                    banks[bank][:, 256 * (h % 2):256 * (h % 2 + 1)], lhsT=ksb[j][:, h, :],
                    rhs=v_tok[:, n, 256 * h:256 * (h + 1)], start=True, stop=True),
                    reads=[("ks", j)], writes=[("pskv", bank, h % 2)], banks=[bank])

        def s_update(d, hs, bank_of):
            for h in hs:
                bank = bank_of(h)
                S.op("dve", lambda e, h=h, bank=bank: e.scalar_tensor_tensor(
                    out=Sst[:, d, h, :], in0=Sst[:, d, h, :], scalar=gC[:, 4 * d + h:4 * d + h + 1],
                    in1=banks[bank][:, 256 * (h % 2):256 * (h % 2 + 1)], op0=ALU.mult, op1=ALU.add),
                    reads=[("pskv", bank, h % 2), ("S", d)], writes=[("S", d)], banks=[bank])

        def pre_mm(n, idx):
            k_scale(n, 1, idx % 2)
            kv_mm(n, idx % 2, range(4), lambda h, idx=idx: PS_O[idx % 2][h // 2])

        pre_mm(NOWN - 1, 0)
        for idx, n in enumerate(range(NOWN - 1, 0, -1)):
            if n - 1 >= 1:
                pre_mm(n - 1, idx + 1)
            S.op("act", lambda e, n=n: e.activation(out=Pb[:, n, :], in_=Sst[:, 1].rearrange("p h e -> p (h e)"),
                                                    func=AF.Copy), reads=[("S", 1)], writes=[("Pb", n)])
            s_update(1, range(4), lambda h, idx=idx: PS_O[idx % 2][h // 2])
        S.op("act", lambda e: e.activation(out=Pb[:, 0, :], in_=Sst[:, 1].rearrange("p h e -> p (h e)"),
                                           func=AF.Copy), reads=[("S", 1)], writes=[("Pb", 0)])

        scl = float(128 ** -0.5)
        PS_QQ = [0, 1]
        PS_SC = PS_TP = 2
        tpz_bf = banks[PS_TP][:].bitcast(BF16).rearrange("p (k t) -> p k t", k=8)

        def proj_block(b):
            tsl = slice(256 * b, 256 * (b + 1))
            for h in range(4):
                qb_ = PS_QQ[h % 2]
                for (c0, off) in ((C_RQ + 128 * h, 0), (C_RK + 128 * h, 256)):
                    for k in range(8):
                        S.op("pe", lambda e, k=k, c0=c0, off=off, qb_=qb_: e.matmul(
                            banks[qb_][:, off:off + 256], lhsT=w_qk[:, k, c0:c0 + 128], rhs=hT_own[:, k, tsl],
                            start=(k == 0), stop=(k == 7)), reads=["w_qk"], writes=[("psqk", h % 2)], banks=[qb_])
                S.op("act", lambda e, h=h, qb_=qb_: e.activation(out=kT[:, h, :], in_=banks[qb_][:, 256:512],
                                                                 func=AF.Copy, scale=scl),
                     reads=[("psqk", h % 2)], writes=[("kT", h)], banks=[qb_])
                S.op("act", lambda e, h=h, qb_=qb_: e.activation(out=qT[:, h, :], in_=banks[qb_][:, 0:256],
                                                                 func=AF.Copy),
                     reads=[("psqk", h % 2)], writes=[("qT", h)], banks=[qb_])
                S.op("dve", lambda e, h=h, qb_=qb_: e.tensor_tensor(
                    out=qfT[:, h, :].rearrange("p (c t) -> p c t", c=2),
                    in0=banks[qb_][:, 0:256].rearrange("p (c t) -> p c t", c=2),
                    in1=Rf[:, h, :].unsqueeze(1).to_broadcast([128, 2, 128]), op=ALU.mult),
                    reads=[("psqk", h % 2), ("Rf", h)], writes=[("qfT", h)], banks=[qb_])
                S.op("dve", lambda e, h=h, qb_=qb_: e.tensor_tensor(
                    out=qbT[:, h, :].rearrange("p (c t) -> p c t", c=2),
                    in0=banks[qb_][:, 0:256].rearrange("p (c t) -> p c t", c=2),
                    in1=Rb[:, h, :].unsqueeze(1).to_broadcast([128, 2, 128]), op=ALU.mult),
                    reads=[("psqk", h % 2), ("Rb", h)], writes=[("qbT", h)], banks=[qb_])

        ksf = [kfs, A(197.5, 1024, BF16, "p (h t) -> p h t", h=4)]

        ksf_key = [("ks", 1), ("ksf", 1)]

        def k_scale_pool(n):
            S.op("pool", lambda e: e.tensor_tensor(
                out=ksf[n % 2], in0=k_tok[:, n, :].rearrange("p (h t) -> p h t", h=4),
                in1=kdec[:, 0:4].unsqueeze(2).to_broadcast([128, 4, 128]), op=ALU.mult),
                reads=["kdec"], writes=[ksf_key[n % 2]] + (["tD1", "tD2"] if n % 2 == 1 else []))

        def kvf_mm(n, hs):
            for h in hs:
                S.op("pe", lambda e, h=h: e.matmul(
                    banks[PS_KV][:, 256 * (h % 2):256 * (h % 2 + 1)], lhsT=ksf[n % 2][:, h, :],
                    rhs=v_tok[:, n, 256 * h:256 * (h + 1)], start=True, stop=True),
                    reads=[ksf_key[n % 2]], writes=[("pskv", PS_KV, h % 2)], banks=[PS_KV])

        def X(n):
            par = n % 2
            csl = slice(128 * (n % 2), 128 * (n % 2 + 1))
            S.op("act", lambda e: e.activation(out=Pfb[par], in_=Sst[:, 0].rearrange("p h e -> p (h e)"),
                                               func=AF.Copy), reads=[("S", 0)],
                 writes=[("Pf", par)] + (["ctab"] if par == 1 else []))
            if n + 1 < NOWN - 1:
                k_scale_pool(n + 1)
            if n < NOWN - 1:
                kvf_mm(n, (0, 1))
            for h in range(4):
                S.op("pe", lambda e, h=h: e.matmul(
                    banks[PS_SC][:, 128 * h:128 * (h + 1)], lhsT=kT[:, h, csl], rhs=qT[:, h, csl],
                    start=True, stop=True), reads=[("kT", h), ("qT", h)], writes=["pssc"], banks=[PS_SC])
            if n < NOWN - 1:
                s_update(0, (0, 1), lambda h: PS_KV)
                kvf_mm(n, (2, 3))
            S.op("dve", lambda e: e.tensor_tensor(
                out=ATb[par], in0=banks[PS_SC][:].rearrange("p (h t) -> p h t", h=4), in1=Dt, op=ALU.mult),
                reads=["pssc"] + Dkeys, writes=[("AT", par)] + ([("ks", 0)] if par == 1 else []),
                banks=[PS_SC])
            if n < NOWN - 1:
                s_update(0, (2, 3), lambda h: PS_KV)

        def Y(n):
            par = n % 2
            csl = slice(128 * (n % 2), 128 * (n % 2 + 1))
            for h in range(4):
                bank = PS_O[par][h // 2]
                osl = slice(256 * (h % 2), 256 * (h % 2 + 1))
                vsl = slice(256 * h, 256 * (h + 1))
                S.op("pe", lambda e, h=h, bank=bank, osl=osl, vsl=vsl: e.matmul(
                    banks[bank][:, osl], lhsT=ATb[par][:, h, :], rhs=v_tok[:, n, vsl], start=True, stop=False),
                    reads=[("AT", par)], writes=[("pso", par, h)], banks=[bank])
                S.op("pe", lambda e, h=h, bank=bank, osl=osl, vsl=vsl: e.matmul(
                    banks[bank][:, osl], lhsT=qfT[:, h, csl], rhs=Pfb[par][:, vsl], start=False, stop=False),
                    reads=[("qfT", h), ("Pf", par)], writes=[("pso", par, h)], banks=[bank])
                S.op("pe", lambda e, h=h, bank=bank, osl=osl, vsl=vsl: e.matmul(
                    banks[bank][:, osl], lhsT=qbT[:, h, csl], rhs=Pb[:, n, vsl], start=False, stop=True),
                    reads=[("qbT", h), ("Pb", n)], writes=[("pso", par, h)], banks=[bank])
            for h in range(4):
                bank = PS_O[par][h // 2]
                osl = slice(256 * (h % 2), 256 * (h % 2 + 1))
                S.op("dve", lambda e, h=h, bank=bank, osl=osl: e.bn_stats(out=st[:, h, :], in_=banks[bank][:, osl]),
                     reads=[("pso", par, h)], writes=[("st", h)], banks=[bank])
                S.op("dve", lambda e, h=h: e.bn_aggr(out=mv[:, h, :], in_=st[:, h, :]),
                     reads=[("st", h)], writes=[("mv", h)])
            mvk = [("mv", h) for h in range(4)]
            S.op("dve", lambda e: e.tensor_scalar(out=rsin, in0=mv[:, :, 1], scalar1=GN_EPS, scalar2=None,
                                                  op0=ALU.add), reads=mvk, writes=["rsin"])
            S.op("pool", lambda e: e.tensor_tensor(out=rs, in0=rsin, in1=cvec[:, 2:3].to_broadcast([128, 4]),
                                                   op=ALU.pow), reads=["rsin"], writes=["rs"])

        def Y2(n):
            par = n % 2
            mvk = [("mv", h) for h in range(4)]
            S.op("dve", lambda e: e.scalar_tensor_tensor(out=nmr, in0=mv[:, :, 0], scalar=-1.0, in1=rs,
                                                         op0=ALU.mult, op1=ALU.mult),
                 reads=mvk + ["rs"], writes=["nmr"])
            for h in range(4):
                bank = PS_O[par][h // 2]
                osl = slice(256 * (h % 2), 256 * (h % 2 + 1))
                S.op("act", lambda e, h=h, bank=bank, osl=osl: e.activation(
                    out=tn[:, 256 * h:256 * (h + 1)], in_=banks[bank][:, osl], func=AF.Identity,
                    scale=rs[:, h:h + 1], bias=nmr[:, h:h + 1]),
                    reads=[("pso", par, h), "rs", "nmr"], writes=[("tn", h)], banks=[bank])

        def Z(n):
            for k in range(8):
                S.op("pe", lambda e, k=k: e.transpose(out=tpz_bf[:, k, :], in_=tn[:, 128 * k:128 * (k + 1)],
                                                      identity=ident),
                     reads=[("tn", k // 2)], writes=["pstp2"], banks=[PS_TP])
            S.op("act", lambda e: e.activation(out=retnT[:, :, 128 * n:128 * (n + 1)], in_=tpz_bf, func=AF.Copy),
                 reads=["pstp2"], writes=[("retnT", n)], banks=[PS_TP])

        k_scale_pool(0)
        proj_block(0)
        X(0)
        for i in range(NOWN):
            Y(i)
            if i + 1 < NOWN:
                if (i + 1) % 2 == 0:
                    proj_block((i + 1) // 2)
                X(i + 1)
            if i >= 1:
                Z(i - 1)
            Y2(i)
        Z(NOWN - 1)
        if stage_limit == 2:
            dbg("retnT", retnT, [128, 8, 2048], BF16)
            dbg("Pb", Pb, [128, 16, 1024], BF16)
            dbg("Sfin", Sst, [128, 2, 4, 256], F32)

    if stage_limit >= 3:
        S.barrier()
        w_rg = A(80, 16384, BF16, "p (k c) -> p k c", k=8)
        w_ro = A(96, 16384, BF16, "p (k c) -> p k c", k=8)
        w_mr = A(112, 16384, BF16, "p (k c) -> p k c", k=8)
        sgb = [A(160 + j, 1024, BF16) for j in range(2)]
        gT = A(162, 8192, BF16, "p (k t) -> p k t", k=8)
        sigb = [A(170 + 2 * j, 2048, F32) for j in range(2)]
        S.dma("pool", lambda e: e.dma_start(out=w_rg, in_=win_v[:, :, C_RG:C_RG + 1024]), writes=["w_rg"])
        S.dma("pool", lambda e: e.dma_start(out=w_ro, in_=wro_v), writes=["w_ro"])
        S.dma("pool", lambda e: e.dma_start(out=w_mr, in_=win_v[:, :, C_MR:C_MR + 1024]), writes=["w_mr"])

        def gate_block(tb, w_g, srcT, srckey, gdst, wkey="w_rg"):
            tsl = slice(512 * tb, 512 * (tb + 1))
            for m in range(8):
                bank = m % 2
                for k in range(8):
                    S.op("pe", lambda e, k=k, m=m, bank=bank: e.matmul(
                        banks[bank][:], lhsT=w_g[:, k, 128 * m:128 * (m + 1)], rhs=hT_own[:, k, tsl],
                        start=(k == 0), stop=(k == 7)), reads=[wkey], writes=[("psg", bank)], banks=[bank])
                S.op("act", lambda e, m=m, bank=bank: e.activation(out=sgb[m % 2], in_=banks[bank][:], func=AF.Silu),
                     reads=[("psg", bank)], writes=[("sg", m % 2)], banks=[bank])
                S.op("dve", lambda e, m=m: e.tensor_tensor(out=gdst[:, m, :], in0=sgb[m % 2], in1=srcT[:, m, tsl],
                                                           op=ALU.mult),
                     reads=[("sg", m % 2), (srckey, m, tb)], writes=[("gT", m)])

        def s2b_block(tb):
            tsl = slice(512 * tb, 512 * (tb + 1))
            gate_block(tb, w_rg, retnT, "retnT_", gT)
            for j in range(8):
                by, bm = 2 + j % 2, 4 + j % 2
                for m in range(8):
                    S.op("pe", lambda e, m=m, j=j, by=by: e.matmul(
                        banks[by][:], lhsT=w_ro[:, m, 128 * j:128 * (j + 1)], rhs=gT[:, m, :],
                        start=(m == 0), stop=(m == 7)), reads=[("gT", m), "w_ro"], writes=[("psy", by)], banks=[by])
                for k in range(8):
                    S.op("pe", lambda e, k=k, j=j, bm=bm: e.matmul(
                        banks[bm][:], lhsT=w_mr[:, k, 128 * j:128 * (j + 1)], rhs=hT_own[:, k, tsl],
                        start=(k == 0), stop=(k == 7)), reads=["w_mr"], writes=[("psm", bm)], banks=[bm])
                S.op("act", lambda e, j=j, bm=bm: e.activation(out=sigb[j % 2], in_=banks[bm][:], func=AF.Sigmoid),
                     reads=[("psm", bm)], writes=[("sig", j % 2)], banks=[bm])
                S.op("dve", lambda e, j=j, by=by: e.tensor_tensor(out=mergedT[:, j, tsl], in0=banks[by][:],
                                                                   in1=sigb[j % 2], op=ALU.mult),
                     reads=[("psy", by), ("sig", j % 2)], writes=[("mergedT", j, tb)], banks=[by])

        for tb in range(4):
            s2b_block(tb)
        if stage_limit == 3:
            dbg("m1T", mergedT, [128, 8, 2048], BF16)

    if stage_limit >= 4:
        S.barrier()
        qT_att = A(80, 32768, BF16, "p (k t) -> p k t", k=8)
        w_aq = A(112, 16384, BF16, "p (k c) -> p k c", k=8)
        kT2 = A(128, 18432, BF16, "p (h t) -> p h t", h=4)
        w_akd = A(146, 8192, BF16, "p (k h c) -> p k h c", k=8, h=4)
        w_av = A(154, 4096, BF16, "p (k c) -> p k c", k=8)
        v_att = A(160, 18432, BF16, "p (e c) -> p e c", e=18)
        sq = A(178, 2048, F32)
        lnt = A(180, 2048, F32)
        rst = [A(182 + 2 * j, 2048, F32) for j in range(2)]
        S.dma("pool", lambda e: e.dma_start(out=w_aq, in_=win_v[:, :, C_AQ:C_AQ + 1024]), writes=["w_aq"])
        for hk_ in range(4):
            for u_ in range(2):
                S.dma("pool", lambda e, hk_=hk_, u_=u_: e.dma_start(
                    out=w_akd[:, :, hk_, 64 * u_:64 * (u_ + 1)],
                    in_=win_v[:, :, C_AK + 64 * hk_:C_AK + 64 * (hk_ + 1)]), writes=[("w_akd", hk_, u_)])
        S.dma("pool", lambda e: e.dma_start(out=w_av, in_=win_v[:, :, C_AV:C_AV + 256]), writes=["w_av"])
        cnt = [0]

        def ext_src(e):
            if e == 0:
                return hT_halo[:, :, 0:128]
            if e == 17:
                return hT_halo[:, :, 128:256]
            return hT_own[:, :, 128 * (e - 1):128 * e]

        sqb = [A(178, 1024, BF16), A(179, 1024, BF16)]
        oblk_bf = A(186, 256, BF16)
        S.op("dve", lambda e: e.tensor_copy(out=oblk_bf, in_=oblk), writes=["oblk_bf"])
        qk_items = []

        def qk_A(it, lhs_fn, rhs_ap, N, dst, dkeys, is_k, wkeys):
            ba = it % 3
            for k in range(8):
                S.op("pe", lambda e, k=k: e.matmul(banks[ba][:, 0:N], lhsT=lhs_fn(k), rhs=rhs_ap[:, k, :],
                                                   start=(k == 0), stop=(k == 7)),
                     reads=wkeys, writes=[("psa", ba)], banks=[ba])
            S.op("act", lambda e: e.activation(out=sqb[it % 2][:, 0:N], in_=banks[ba][:, 0:N], func=AF.Square),
                 reads=[("psa", ba)], writes=[("sq", it % 2)], banks=[ba])

        def qk_B(it, lhs_fn, rhs_ap, N, dst, dkeys, is_k, wkeys):
            ba, bs = it % 3, 3 + it % 2
            S.op("pe", lambda e: e.matmul(banks[bs][:, 0:N], lhsT=oblk_bf, rhs=sqb[it % 2][:, 0:N], start=True, stop=True),
                 reads=[("sq", it % 2), "oblk_bf"], writes=[("pss", bs)], banks=[bs])
            S.op("act", lambda e: e.activation(out=lnt[:, 0:N], in_=banks[bs][:, 0:N], func=AF.Ln,
                                               scale=1.0 / 64, bias=float(RMS_EPS)),
                 reads=[("pss", bs)], writes=["lnt"], banks=[bs])
            S.op("act", lambda e: e.activation(out=rst[it % 2][:, 0:N], in_=lnt[:, 0:N], func=AF.Exp, scale=-0.5),
                 reads=["lnt"], writes=[("rst", it % 2)])
            if is_k:
                S.op("dve", lambda e: e.scalar_tensor_tensor(out=dst, in0=banks[ba][:, 0:N], scalar=gkq[:, 0:1],
                                                             in1=rst[it % 2][:, 0:N], op0=ALU.mult, op1=ALU.mult),
                     reads=[("psa", ba), ("rst", it % 2)], writes=dkeys, banks=[ba])
            else:
                S.op("dve", lambda e: e.tensor_tensor(out=dst, in0=banks[ba][:, 0:N], in1=rst[it % 2][:, 0:N],
                                                      op=ALU.mult),
                     reads=[("psa", ba), ("rst", it % 2)], writes=dkeys, banks=[ba])

        def qknorm(*args):
            qk_items.append(args)

        epsb = A(46.125, 4, F32)
        S.op("dve", lambda e: e.memset(epsb, RMS_EPS), writes=["epsb"])
        for m in range(8):
            for tb in range(4):
                qknorm(lambda k, m=m: w_aq[:, k, 128 * m:128 * (m + 1)], hT_own[:, :, 512 * tb:512 * (tb + 1)], 512,
                       qT_att[:, m, 512 * tb:512 * (tb + 1)], [("qatt", m, qb) for qb in range(4 * tb, 4 * tb + 4)],
                       False, ["w_aq"])
        for hk in range(4):
            segs = [(hT_halo[:, :, 0:128], 128, 0), (hT_halo[:, :, 128:256], 128, 17 * 128)]
            segs += [(hT_own[:, :, 512 * tb:512 * (tb + 1)], 512, 128 + 512 * tb) for tb in range(4)]
            for (rhs_ap, N, t0) in segs:
                qknorm(lambda k, hk=hk: w_akd[:, k, hk, :], rhs_ap, N, kT2[:, hk, t0:t0 + N],
                       [("kT2", hk, (t0 + 128 * q) // 128) for q in range(N // 128)], True,
                       [("w_akd", hk, 0), ("w_akd", hk, 1)])
        qk_A(0, *qk_items[0])
        for it_ in range(len(qk_items)):
            if it_ + 1 < len(qk_items):
                qk_A(it_ + 1, *qk_items[it_ + 1])
            qk_B(it_, *qk_items[it_])
        for ex in range(18):
            bv = 5 + ex % 2
            srcx = ext_src(ex)
            for k in range(8):
                S.op("pe", lambda e, k=k, bv=bv, srcx=srcx: e.matmul(
                    banks[bv][:, 0:256], lhsT=srcx[:, k, :], rhs=w_av[:, k, :], start=(k == 0), stop=(k == 7)),
                    reads=["w_av"], writes=[("psv", bv)], banks=[bv])
            vview = v_att[:, ex, :].rearrange("p (h u c) -> p h u c", h=4, u=2)
            for u in range(1):
                if ex in (0, 17):
                    sc = halov[:, 0:1] if ex == 0 else halov[:, 1:2]
                    S.op("act", lambda e, u=u, bv=bv, vview=vview, sc=sc: e.activation(
                        out=vview[:, :, u, :], in_=banks[bv][:, 0:256].rearrange("p (h c) -> p h c", h=4),
                        func=AF.Copy, scale=sc), reads=[("psv", bv)], writes=[("v_att", ex, u)], banks=[bv])
                else:
                    S.op("act", lambda e, u=u, bv=bv, vview=vview: e.activation(
                        out=vview[:, :, u, :], in_=banks[bv][:, 0:256].rearrange("p (h c) -> p h c", h=4),
                        func=AF.Copy), reads=[("psv", bv)], writes=[("v_att", ex, u)], banks=[bv])
        if stage_limit == 4:
            dbg("qT_att", qT_att, [128, 8, 2048], BF16)
            dbg("kT2", kT2, [128, 4, 2304], BF16)
            dbg("v_att", v_att, [128, 18, 512], BF16)

    if stage_limit >= 5:
        S.barrier()
        Etab = A(180, 24576, F32, "p (h b t) -> p h b t", h=16, b=3)
        actab = A(112, 3072, F32, "p (n c) -> p n c", n=6)
        etmp = [A(115 + 3 * j, 3072, F32, "p (u c) -> p u c", u=2) for j in range(2)]
        PT = [A(121 + 1.5 * j, 1536, BF16, "p (b u t) -> p b u t", b=3, u=2) for j in range(2)]
        densb = [A(124 + j, 1024, F32) for j in range(2)]
        recsb = [A(126 + j, 1024, F32) for j in range(2)]
        etb = A(178, 512, F32)
        onef = A(46.1875, 4, F32)
        S.op("dve", lambda e: e.memset(onef, 1.0), writes=["onef"])
        S.dma("sp", lambda e: e.dma_start(
            out=actab, in_=const_d[:, cidx["ar0"] * 128:(cidx["ar0"] + 6) * 128].rearrange("p (n c) -> p n c", n=6)),
            writes=["actab"])
        for hq in range(16):
            slope = float(2.0 ** (-8.0 * (hq + 1.0) / 16))
            S.op("act", lambda e, hq=hq, slope=slope: e.activation(out=Etab[:, hq, :, :], in_=actab[:, 0:3, :],
                                                                   func=AF.Exp, scale=-slope,
                                                                   bias=negsink[:, hq:hq + 1]),
                 reads=["actab"], writes=[("Etab", hq)])
            S.op("dve", lambda e, hq=hq: e.tensor_tensor(out=Etab[:, hq, :, :], in0=Etab[:, hq, :, :],
                                                         in1=actab[:, 3:6, :], op=ALU.mult),
                 reads=[("Etab", hq), "actab"], writes=[("Etab", hq)])
        itc = [0]

        def att_A(it, qb, m):
            hk = m // 2
            se, so = 2 * (it % 2), 2 * (it % 2) + 1
            qsl = slice(128 * qb, 128 * (qb + 1))
            for kb in range(3):
                ex = qb + kb
                ksl = slice(128 * ex, 128 * (ex + 1))
                S.op("pe", lambda e, kb=kb, ksl=ksl: e.matmul(
                    banks[se][:, 128 * kb:128 * (kb + 1)], lhsT=kT2[0:64, hk, ksl], rhs=qT_att[0:64, m, qsl],
                    start=True, stop=True), reads=[("kT2", hk, ex), ("qatt", m, qb)], writes=[("pse", se)],
                    banks=[se])
                S.op("pe", lambda e, kb=kb, ksl=ksl: e.matmul(
                    banks[so][:, 128 * kb:128 * (kb + 1)], lhsT=kT2[64:128, hk, ksl], rhs=qT_att[64:128, m, qsl],
                    start=True, stop=True), reads=[("kT2", hk, ex), ("qatt", m, qb)], writes=[("pse", so)],
                    banks=[so])
            for u, bk in ((0, se), (1, so)):
                S.op("act", lambda e, u=u, bk=bk: e.activation(
                    out=etmp[it % 2][:, u, :], in_=banks[bk][:, 0:384], func=AF.Exp),
                    reads=[("pse", bk)], writes=[("etmp", it % 2, u)], banks=[bk])
                S.op("dve", lambda e, u=u: e.tensor_tensor(
                    out=PT[it % 2][:, :, u, :], in0=etmp[it % 2][:, u, :].rearrange("p (b t) -> p b t", b=3),
                    in1=Etab[:, 2 * m + u, :, :], op=ALU.mult),
                    reads=[("etmp", it % 2, u), ("Etab", 2 * m + u)], writes=[("PT", it % 2, u)])

        def att_B(it, qb, m):
            hk = m // 2
            bpv, bden = 4 + it % 2, 6 + it % 2
            qsl = slice(128 * qb, 128 * (qb + 1))
            for kb in range(3):
                ex = qb + kb
                ones_x = ones_bf
                if qb == 0 and kb == 0:
                    ones_x = ones_vb
                if qb == NOWN - 1 and kb == 2:
                    ones_x = ones_va
                for u in range(2):
                    psl = slice(64 * u, 64 * (u + 1))
                    S.op("pe", lambda e, kb=kb, ex=ex, u=u, psl=psl: e.matmul(
                        banks[bpv][psl, 0:128], lhsT=v_att[:, ex, 128 * hk:128 * hk + 64], rhs=PT[it % 2][:, kb, u, :],
                        start=(kb == 0), stop=(kb == 2)),
                        reads=[("PT", it % 2, u), ("v_att", ex, 0)], writes=[("pspv", bpv)], banks=[bpv])
                    S.op("pe", lambda e, kb=kb, u=u, psl=psl, ones_x=ones_x: e.matmul(
                        banks[bden][psl, 0:128], lhsT=ones_x[:, 0:64], rhs=PT[it % 2][:, kb, u, :],
                        start=(kb == 0), stop=(kb == 2)),
                        reads=[("PT", it % 2, u)], writes=[("psden", bden)], banks=[bden])
            S.op("act", lambda e: e.activation(out=densb[it % 2][:, 0:128], in_=banks[bden][:, 0:128], func=AF.Ln,
                                               bias=1.0),
                 reads=[("psden", bden)], writes=[("den", it % 2)], banks=[bden])
            S.op("act", lambda e: e.activation(out=recsb[it % 2][:, 0:128], in_=densb[it % 2][:, 0:128], func=AF.Exp,
                                               scale=-1.0),
                 reads=[("den", it % 2)], writes=[("rec", it % 2)])
            S.op("dve", lambda e: e.tensor_tensor(out=qT_att[:, m, qsl], in0=banks[bpv][:, 0:128],
                                                  in1=recsb[it % 2][:, 0:128], op=ALU.mult),
                 reads=[("pspv", bpv), ("rec", it % 2)], writes=[("qatt", m, qb)], banks=[bpv])

        its = [(qb, m) for qb in range(NOWN) for m in range(8)]
        att_A(0, *its[0])
        for it in range(len(its)):
            if it + 1 < len(its):
                att_A(it + 1, *its[it + 1])
            att_B(it, *its[it])
        if stage_limit == 5:
            dbg("attT", qT_att, [128, 8, 2048], BF16)
            dbg("m1T", mergedT, [128, 8, 2048], BF16)

    if stage_limit >= 6:
        S.barrier()
        w_ag = A(128, 16384, BF16, "p (k c) -> p k c", k=8)
        w_ao = A(144, 16384, BF16, "p (k c) -> p k c", k=8)
        w_ma = A(160, 16384, BF16, "p (k c) -> p k c", k=8)
        sgb3 = [A(176 + j, 1024, BF16) for j in range(2)]
        gT3 = A(178, 8192, BF16, "p (k t) -> p k t", k=8)
        sigb3 = [A(186 + 2 * j, 2048, F32) for j in range(2)]
        tmpm = [A(190 + 2 * j, 1024, BF16) for j in range(2)]
        S.dma("pool", lambda e: e.dma_start(out=w_ag, in_=win_v[:, :, C_AG:C_AG + 1024]), writes=["w_ag"])
        S.dma("pool", lambda e: e.dma_start(out=w_ao, in_=wao_v), writes=["w_ao"])
        S.dma("pool", lambda e: e.dma_start(out=w_ma, in_=win_v[:, :, C_MA:C_MA + 1024]), writes=["w_ma"])
        w_o = A(112, 16384, BF16, "p (k c) -> p k c", k=8)
        S.dma("pool", lambda e: e.dma_start(out=w_o, in_=wout_v), writes=["w_o_pre"])

        def s3c_block(tb):
            tsl = slice(512 * tb, 512 * (tb + 1))
            for m in range(8):
                bank = m % 2
                for k in range(8):
                    S.op("pe", lambda e, k=k, m=m, bank=bank: e.matmul(
                        banks[bank][:], lhsT=w_ag[:, k, 128 * m:128 * (m + 1)], rhs=hT_own[:, k, tsl],
                        start=(k == 0), stop=(k == 7)), reads=["w_ag"], writes=[("psg", bank)], banks=[bank])
                S.op("act", lambda e, m=m, bank=bank: e.activation(out=sgb3[m % 2], in_=banks[bank][:], func=AF.Silu),
                     reads=[("psg", bank)], writes=[("sg", m % 2)], banks=[bank])
                S.op("dve", lambda e, m=m: e.tensor_tensor(out=gT3[:, m, :], in0=sgb3[m % 2], in1=qT_att[:, m, tsl],
                                                           op=ALU.mult),
                     reads=[("sg", m % 2)], writes=[("gT3", m)])
            for j in range(8):
                by, bm = 2 + j % 2, 4 + j % 2
                for m in range(8):
                    S.op("pe", lambda e, m=m, j=j, by=by: e.matmul(
                        banks[by][:], lhsT=w_ao[:, m, 128 * j:128 * (j + 1)], rhs=gT3[:, m, :],
                        start=(m == 0), stop=(m == 7)), reads=[("gT3", m), "w_ao"], writes=[("psy", by)], banks=[by])
                for k in range(8):
                    S.op("pe", lambda e, k=k, j=j, bm=bm: e.matmul(
                        banks[bm][:], lhsT=w_ma[:, k, 128 * j:128 * (j + 1)], rhs=hT_own[:, k, tsl],
                        start=(k == 0), stop=(k == 7)), reads=["w_ma"], writes=[("psm", bm)], banks=[bm])
                S.op("act", lambda e, j=j, bm=bm: e.activation(out=sigb3[j % 2], in_=banks[bm][:], func=AF.Sigmoid),
                     reads=[("psm", bm)], writes=[("sig", j % 2)], banks=[bm])
                S.op("dve", lambda e, j=j, by=by: e.tensor_tensor(out=tmpm[j % 2], in0=banks[by][:],
                                                                   in1=sigb3[j % 2], op=ALU.mult),
                     reads=[("psy", by), ("sig", j % 2)], writes=[("tmpm", j % 2)], banks=[by])
                S.op("dve", lambda e, j=j: e.tensor_tensor(out=mergedT[:, j, tsl], in0=mergedT[:, j, tsl],
                                                           in1=tmpm[j % 2], op=ALU.add),
                     reads=[("tmpm", j % 2)], writes=[("mergedT", j, tb)])

        for tb in range(4):
            s3c_block(tb)
        if stage_limit == 6:
            dbg("mergedT", mergedT, [128, 8, 2048], BF16)

    if stage_limit >= 7:
        S.barrier()
        xt = [A(128 + 4 * j, 4096, F32) for j in range(NOWN)]
        ot = [A(80 + 4 * j, 4096, F32) for j in range(4)]
        for i_ in range(NOWN):
            S.dma("sp", lambda e, i_=i_: e.dma_start(out=xt[i_], in_=x_d[128 * i_:128 * (i_ + 1), :]),
                  writes=[("xt", i_)])

        def out_tile(i):
            for half in range(2):
                bank = (2 * i + half) % 8
                for j in range(8):
                    S.op("pe", lambda e, j=j, bank=bank, half=half: e.matmul(
                        banks[bank][:], lhsT=mergedT[:, j, 128 * i:128 * (i + 1)],
                        rhs=w_o[:, j, 512 * half:512 * (half + 1)], start=(j == 0), stop=(j == 7)),
                        reads=[], writes=[("pso5", bank)], banks=[bank])
                S.op("dve", lambda e, bank=bank, half=half: e.tensor_tensor(
                    out=ot[i % 4][:, 512 * half:512 * (half + 1)], in0=banks[bank][:],
                    in1=xt[i][:, 512 * half:512 * (half + 1)], op=ALU.add),
                    reads=[("pso5", bank), ("xt", i)], writes=[("ot", i % 4, half)], banks=[bank])
            S.dma("sp", lambda e: e.dma_start(out=out_d[128 * i:128 * (i + 1), :], in_=ot[i % 4]),
                  reads=[("ot", i % 4, 0), ("ot", i % 4, 1)], writes=[("out", i)])
            outs_written.append(("out", i))

        for i in range(NOWN):
            out_tile(i)

    S.barrier()
    fin_keys = []
    for (name, ap, t) in dbg_list:
        S.dma("sp", lambda e, ap=ap, t=t: e.dma_start(out=t, in_=ap), writes=[("dbgout", name)])
        fin_keys.append(("dbgout", name))
    if stage_limit < 7:
        S.dma("sp", lambda e: e.dma_start(out=out_d[0:128, :], in_=xbuf[0]), writes=[("out", 0)])
        fin_keys.append(("out", 0))
    S.op("sp", lambda e: e.nop(), reads=fin_keys + outs_written)
    S.finalize()
    semmap = {}
    for k in S.sem_keys():
        semmap[k] = es.enter_context(nc.semaphore("s_" + "_".join(str(t) for t in k)))
    with nc.Block() as block:
        @block.tensor
        def _(e):
            S.emit_engine("pe", e, semmap)

        @block.scalar
        def _(e):
            S.emit_engine("act", e, semmap)

        @block.vector
        def _(e):
            S.emit_engine("dve", e, semmap)

        @block.gpsimd
        def _(e):
            S.emit_engine("pool", e, semmap)

        @block.sync
        def _(e):
            S.emit_engine("sp", e, semmap)
    es.close()
    return nc, [n for (n, _, _) in dbg_list]


def make_in_maps(x, norm_g, w_in, ret_log_decay, q_norm_g, k_norm_g, attn_sink, w_ret_o, w_attn_o, w_out):
    x2 = np.ascontiguousarray(np.asarray(x, np.float32)[0])
    small = np.zeros((128, NSMALL), np.float32)
    small[:, SM_NG:SM_NG + 1024] = np.asarray(norm_g, np.float32).reshape(1, 1024)
    small[:, SM_LG:SM_LG + 8] = np.asarray(ret_log_decay, np.float32).reshape(1, 8)
    small[:, SM_SINK:SM_SINK + 16] = np.asarray(attn_sink, np.float32).reshape(1, 16)
    small[:, SM_GQ] = np.tile(np.asarray(q_norm_g, np.float32).reshape(64), 2)
    small[:, SM_GK] = np.tile(np.asarray(k_norm_g, np.float32).reshape(64), 2)
    w_in2 = np.ascontiguousarray(np.asarray(w_in, np.float32)[0])
    wro = np.ascontiguousarray(np.asarray(w_ret_o, np.float32)[0])
    wao = np.ascontiguousarray(np.asarray(w_attn_o, np.float32)[0])
    wout = np.ascontiguousarray(np.asarray(w_out, np.float32)[0])
    maps = []
    for c in range(NCORES):
        maps.append({
            "x": np.ascontiguousarray(np.roll(x2, -TOK * c, axis=0)),
            "w_in": w_in2, "w_ret_o": wro, "w_attn_o": wao, "w_out": wout,
            "small": small, "consts": CONST_ARR, "cvec": CONST_VEC, "cmask": _core_masks(c),
        })
    return maps


_NC_CACHE = {}


def kernel(x, norm_g, w_in, ret_log_decay, q_norm_g, k_norm_g, attn_sink, w_ret_o, w_attn_o, w_out):
    if "nc" not in _NC_CACHE:
        _NC_CACHE["nc"] = build()[0]
    nc = _NC_CACHE["nc"]
    maps = make_in_maps(x, norm_g, w_in, ret_log_decay, q_norm_g, k_norm_g, attn_sink, w_ret_o, w_attn_o, w_out)
    res = run_bass_kernel_spmd(nc, maps, core_ids=list(range(NCORES)))
    out = np.concatenate([np.asarray(res.results[c]["out"], np.float32) for c in range(NCORES)], axis=0)
    return out.reshape(1, SEQ, DM)
```
